# Optimizing a Trainium2 kernel written in Bass

```python
import math
import jax, jax.numpy as jnp
from jax import lax
import numpy as np

D_MODEL = 1024
BATCH = 32
SEQ = 256
DEPTH = 2
DEC_BATCH = 2
DEC_SEQ = 2048
PAST_LEN = 256

GRID_W = 64
N_EVEN = (DEPTH + 1) // 2
N_ODD = DEPTH // 2
ML_HEADS = 4
ML_DK = 128
ML_DV = 128
ML_WIDTH = ML_HEADS * ML_DV
ML_CHUNK = 64
HG_HEADS = 4
HG_DK = 128
HG_DV = 128
HG_WIDTH = HG_HEADS * HG_DK
HG_CHUNK = 32
DA_HEADS = 8
DA_DQK = 64
DA_DV = 2 * DA_DQK
DA_WIDTH = DA_HEADS * DA_DV
Q_BLOCK = 128
ROPE_BASE = 10000.0
EPS = 1e-6
D_FF = ((8 * D_MODEL // 3 + 255) // 256) * 256
EV_SIZES = (ML_HEADS * ML_DK, ML_HEADS * ML_DK, ML_WIDTH, ML_WIDTH, 4 * ML_HEADS,
            HG_WIDTH, HG_WIDTH, HG_WIDTH, HG_HEADS * HG_DV, HG_HEADS * HG_DV)
EV_IN = sum(EV_SIZES)
OD_SIZES = (DA_HEADS * 2 * DA_DQK, DA_HEADS * 2 * DA_DQK, DA_WIDTH)
OD_IN = sum(OD_SIZES)

kernel_name = 'hybrid_mlstm_hgrn2_diffattn_prefix_dit'

f32 = jnp.float32


def _split(x, sizes):
    idx, acc = [], 0
    for s in sizes[:-1]:
        acc += s
        idx.append(acc)
    return jnp.split(x, idx, axis=-1)


def _rmsnorm(x, g):
    xf = x.astype(f32)
    return xf * lax.rsqrt(jnp.mean(xf * xf, axis=-1, keepdims=True) + EPS) * g.astype(f32)


def _head_rmsnorm(x, n_heads, g):
    B, S, W = x.shape
    xh = x.reshape(B, S, n_heads, W // n_heads)
    xh = xh * lax.rsqrt(jnp.mean(xh * xh, axis=-1, keepdims=True) + EPS)
    return xh.reshape(B, S, W) * g.astype(f32)


def _heads(x, n):
    B, S, _ = x.shape
    return x.astype(f32).reshape(B, S, n, -1).transpose(0, 2, 1, 3)


def _merge(x):
    B, H, S, d = x.shape
    return x.transpose(0, 2, 1, 3).reshape(B, S, H * d)


def _chunks(t, L):
    B, H, S = t.shape[:3]
    t = t.astype(f32).reshape((B, H, S // L, L) + t.shape[3:])
    return jnp.moveaxis(t, 2, 0)


def _unchunk(y):
    nc, B, H, L = y.shape[:4]
    y = jnp.moveaxis(y, 0, 2)
    return y.reshape((B, H, nc * L) + y.shape[4:])


def _flip(t):
    return jnp.flip(t, axis=2)


def _mlstm_chunkwise(q, k, v, ig, lf, C0, n0, m0):
    L = ML_CHUNK
    tri = jnp.tril(jnp.ones((L, L), bool))

    def step(carry, xs):
        C, n, m = carry
        qc, kc, vc, ic, fc = xs
        b = jnp.cumsum(fc, axis=-1)
        dmat = jnp.where(tri, b[..., :, None] - b[..., None, :] + ic[..., None, :], -jnp.inf)
        inter = b + m[..., None]
        mt = jnp.maximum(inter, jnp.max(dmat, axis=-1))
        w_inter = jnp.exp(inter - mt)
        s = jnp.einsum('bhtd,bhsd->bhts', qc, kc) * jnp.exp(dmat - mt[..., None])
        num = w_inter[..., None] * jnp.einsum('bhtd,bhde->bhte', qc, C) + jnp.einsum('bhts,bhse->bhte', s, vc)
        den = w_inter * jnp.einsum('bhtd,bhd->bht', qc, n) + jnp.sum(s, axis=-1)
        h = num / jnp.maximum(jnp.abs(den), jnp.exp(-mt))[..., None]
        bl = b[..., -1]
        g = bl[..., None] - b + ic
        m_new = jnp.maximum(bl + m, jnp.max(g, axis=-1))
        decay = jnp.exp(bl + m - m_new)
        wg = jnp.exp(g - m_new[..., None])
        C_new = decay[..., None, None] * C + jnp.einsum('bhs,bhsd,bhse->bhde', wg, kc, vc)
        n_new = decay[..., None] * n + jnp.einsum('bhs,bhsd->bhd', wg, kc)
        return (C_new, n_new, m_new), h

    xs = tuple(_chunks(t, L) for t in (q, k, v, ig, lf))
    (C, n, m), h = lax.scan(step, (C0.astype(f32), n0.astype(f32), m0.astype(f32)), xs)
    return _unchunk(h), C, n, m


def _hgrn2_chunkwise(q, lf, i, S0):
    L = HG_CHUNK
    tri = jnp.tril(jnp.ones((L, L), bool))[..., None]

    def step(S, xs):
        qc, fc, ic = xs
        a = jnp.cumsum(fc, axis=2)
        kc = -jnp.expm1(fc)
        inter = jnp.einsum('bhtd,bhde->bhte', qc * jnp.exp(a), S)
        rel = jnp.exp(jnp.where(tri, a[:, :, :, None, :] - a[:, :, None, :, :], -jnp.inf))
        sc = jnp.einsum('bhtd,bhtsd,bhsd->bhts', qc, rel, kc)
        o = inter + jnp.einsum('bhts,bhse->bhte', sc, ic)
        al = a[:, :, -1]
        S_new = jnp.exp(al)[..., None] * S + jnp.einsum('bhsd,bhse->bhde', kc * jnp.exp(al[:, :, None, :] - a), ic)
        return S_new, o

    xs = tuple(_chunks(t, L) for t in (q, lf, i))
    S, o = lax.scan(step, S0.astype(f32), xs)
    return _unchunk(o), S


def _axial_rope(x):
    B, H, N, _ = x.shape
    rows = N // GRID_W
    row = jnp.repeat(jnp.arange(rows, dtype=f32), GRID_W)
    col = jnp.tile(jnp.arange(GRID_W, dtype=f32), rows)
    half = DA_DQK // 2
    quarter = half // 2
    inv = ROPE_BASE ** (-jnp.arange(quarter, dtype=f32) / quarter)
    xr = x.astype(f32).reshape(B, H, N, 2, DA_DQK)

    def rot(xa, pos):
        ang = (pos[:, None] * inv[None, :])[:, None, :]
        cos, sin = jnp.cos(ang), jnp.sin(ang)
        x1, x2 = xa[..., :quarter], xa[..., quarter:]
        return jnp.concatenate([x1 * cos - x2 * sin, x1 * sin + x2 * cos], axis=-1)

    out = jnp.concatenate([rot(xr[..., :half], row), rot(xr[..., half:], col)], axis=-1)
    return out.reshape(B, H, N, 2 * DA_DQK)


def _diff_attention(q, k, v, lam):
    B, H, S, _ = q.shape
    nb = S // Q_BLOCK
    qb = jnp.moveaxis(q.astype(f32).reshape(B, H, nb, Q_BLOCK, 2 * DA_DQK), 2, 0)
    k = k.astype(f32)
    v = v.astype(f32)
    k1, k2 = k[..., :DA_DQK], k[..., DA_DQK:]
    scale = DA_DQK ** -0.5

    def blk(qblk):
        s1 = jnp.einsum('bhqd,bhkd->bhqk', qblk[..., :DA_DQK], k1) * scale
        s2 = jnp.einsum('bhqd,bhkd->bhqk', qblk[..., DA_DQK:], k2) * scale
        a = jax.nn.softmax(s1, axis=-1) - lam * jax.nn.softmax(s2, axis=-1)
        return jnp.einsum('bhqk,bhkd->bhqd', a, v)

    o = lax.map(blk, qb)
    return jnp.moveaxis(o, 0, 2).reshape(B, H, S, DA_DV)


def _modulation(cond, w, b):
    mod = jnp.einsum('...d,de->...e', jax.nn.silu(cond.astype(f32)), w.astype(f32)) + b.astype(f32)
    return jnp.split(mod[..., None, :], 6, axis=-1)


def _even_mixer(h, e, ev_w_in, ev_gate_b, ev_lb_logits, ml_norm_g, hg_norm_g, ev_w_out, C0, n0, m0, S0):
    B, S, _ = h.shape
    proj = jnp.einsum('bsd,de->bse', h, ev_w_in[e])
    ml_q, ml_k, ml_v, ml_o, ml_g, hg_q, hg_ff, hg_fb, hg_i, hg_o = _split(proj, EV_SIZES)
    q = _heads(ml_q, ML_HEADS) * (ML_DK ** -0.5)
    k = _heads(ml_k, ML_HEADS)
    v = _heads(ml_v, ML_HEADS)
    gates = (ml_g + ev_gate_b[e]).astype(f32).reshape(B, S, 4, ML_HEADS).transpose(2, 0, 3, 1)
    ig_f, ig_b, fg_f, fg_b = gates[0], gates[1], gates[2], gates[3]
    hf, Cf, nf, mf = _mlstm_chunkwise(q, k, v, ig_f, jax.nn.log_sigmoid(fg_f), C0[:, 0], n0[:, 0], m0[:, 0])
    hb, Cb, nb, mb = _mlstm_chunkwise(_flip(q), _flip(k), _flip(v), _flip(ig_b), _flip(jax.nn.log_sigmoid(fg_b)),
                                      C0[:, 1], n0[:, 1], m0[:, 1])
    ml_out = _head_rmsnorm(_merge(hf + _flip(hb)), ML_HEADS, ml_norm_g[e]) * jax.nn.sigmoid(ml_o.astype(f32))
    lb = jnp.cumsum(jax.nn.softmax(ev_lb_logits.astype(f32), axis=0), axis=0)[e]
    lf_f = _heads(jnp.log(lb + (1.0 - lb) * jax.nn.sigmoid(hg_ff.astype(f32))), HG_HEADS)
    lf_b = _heads(jnp.log(lb + (1.0 - lb) * jax.nn.sigmoid(hg_fb.astype(f32))), HG_HEADS)
    qh = _heads(hg_q, HG_HEADS)
    ih = _heads(hg_i, HG_HEADS)
    of, Sf = _hgrn2_chunkwise(qh, lf_f, ih, S0[:, 0])
    ob, Sb = _hgrn2_chunkwise(_flip(qh), _flip(lf_b), _flip(ih), S0[:, 1])
    hg_out = _head_rmsnorm(_merge(of + _flip(ob)), HG_HEADS, hg_norm_g[e]) * jax.nn.silu(hg_o.astype(f32))
    out = jnp.einsum('bse,ed->bsd', jnp.concatenate([ml_out, hg_out], axis=-1), ev_w_out[e])
    states = (jnp.stack([Cf, Cb], axis=1), jnp.stack([nf, nb], axis=1),
              jnp.stack([mf, mb], axis=1), jnp.stack([Sf, Sb], axis=1))
    return out, states


def _odd_mixer(h, o, layer_idx, od_w_in, od_lambda, da_norm_g, od_w_out, ctx_k, ctx_v):
    proj = jnp.einsum('bsd,de->bse', h, od_w_in[o])
    q, k, v = _split(proj, OD_SIZES)
    q = _heads(q, DA_HEADS)
    k = _heads(k, DA_HEADS)
    v = _heads(v, DA_HEADS)
    lam_init = 0.8 - 0.6 * math.exp(-0.3 * layer_idx)
    lp = od_lambda[o].astype(f32)
    lam = jnp.exp(jnp.sum(lp[0] * lp[1])) - jnp.exp(jnp.sum(lp[2] * lp[3])) + lam_init
    if ctx_k is None:
        att = _diff_attention(q, k, v, lam)
        cache = (k.transpose(0, 2, 1, 3), v.transpose(0, 2, 1, 3))
    else:
        q = _axial_rope(q)
        k = _axial_rope(k)
        kk = jnp.concatenate([ctx_k.astype(f32).transpose(0, 2, 1, 3), k], axis=2)
        vv = jnp.concatenate([ctx_v.astype(f32).transpose(0, 2, 1, 3), v], axis=2)
        att = _diff_attention(q, kk, vv, lam)
        cache = None
    att = _merge(att)
    att = _head_rmsnorm(att, DA_HEADS, jnp.tile(da_norm_g[o], DA_HEADS)) * (1.0 - lam_init)
    return jnp.einsum('bse,ed->bsd', att, od_w_out[o]), cache


def _swiglu(h, w1, w3, w2):
    return jnp.einsum('bsf,fd->bsd', jax.nn.silu(jnp.einsum('bsd,df->bsf', h, w1)) * jnp.einsum('bsd,df->bsf', h, w3), w2)


def setup_inputs(seed: int = 0) -> dict:
    key = jax.random.key(seed)
    ks = jax.random.split(key, 32)
    D = D_MODEL

    def nrm(k, shape, scale=1.0):
        return scale * jax.random.normal(k, shape, f32)

    gate_i = nrm(ks[10], (N_EVEN, 2 * ML_HEADS), 0.1)
    gate_f = 3.0 + 3.0 * jax.random.uniform(ks[11], (N_EVEN, 2 * ML_HEADS), f32)
    lb_logits = 2.0 * jnp.arange(N_EVEN + 1, dtype=f32)[:, None] + nrm(ks[12], (N_EVEN + 1, HG_WIDTH), 0.1)
    return {
        'x_prompt': nrm(ks[0], (BATCH, SEQ, D)),
        'x_sample': nrm(ks[1], (DEC_BATCH, DEC_SEQ, D)),
        'c': nrm(ks[2], (DEC_BATCH, D)),
        'c_ctx': nrm(ks[3], (D,)),
        'cache_attn_k': nrm(ks[4], (DEC_BATCH, N_ODD, PAST_LEN, DA_HEADS, 2 * DA_DQK)),
        'cache_attn_v': nrm(ks[5], (DEC_BATCH, N_ODD, PAST_LEN, DA_HEADS, DA_DV)),
        'state_mlstm_C': nrm(ks[6], (DEC_BATCH, N_EVEN, 2, ML_HEADS, ML_DK, ML_DV), 0.3),
        'state_mlstm_n': nrm(ks[7], (DEC_BATCH, N_EVEN, 2, ML_HEADS, ML_DK), 0.3),
        'state_mlstm_m': nrm(ks[8], (DEC_BATCH, N_EVEN, 2, ML_HEADS), 0.5),
        'state_hgrn_S': nrm(ks[9], (DEC_BATCH, N_EVEN, 2, HG_HEADS, HG_DK, HG_DV), 0.5),
        'ada_w': nrm(ks[13], (DEPTH, D, 6 * D), 0.3 * D ** -0.5),
        'ada_b': nrm(ks[14], (DEPTH, 6 * D), 0.05),
        'norm_mix_g': 1.0 + nrm(ks[15], (DEPTH, D), 0.05),
        'norm_ffn_g': 1.0 + nrm(ks[16], (DEPTH, D), 0.05),
        'ev_w_in': nrm(ks[17], (N_EVEN, D, EV_IN), D ** -0.5),
        'ev_gate_b': jnp.concatenate([gate_i, gate_f], axis=1),
        'ev_lb_logits': lb_logits,
        'ml_norm_g': 1.0 + nrm(ks[18], (N_EVEN, ML_WIDTH), 0.05),
        'hg_norm_g': 1.0 + nrm(ks[19], (N_EVEN, HG_WIDTH), 0.05),
        'ev_w_out': nrm(ks[20], (N_EVEN, ML_WIDTH + HG_WIDTH, D), (ML_WIDTH + HG_WIDTH) ** -0.5),
        'od_w_in': nrm(ks[21], (N_ODD, D, OD_IN), D ** -0.5),
        'od_lambda': nrm(ks[22], (N_ODD, 4, DA_DQK), 0.1),
        'da_norm_g': 1.0 + nrm(ks[23], (N_ODD, DA_DV), 0.05),
        'od_w_out': nrm(ks[24], (N_ODD, DA_WIDTH, D), DA_WIDTH ** -0.5),
        'ffn_w1': nrm(ks[25], (DEPTH, D, D_FF), D ** -0.5),
        'ffn_w3': nrm(ks[26], (DEPTH, D, D_FF), D ** -0.5),
        'ffn_w2': nrm(ks[27], (DEPTH, D_FF, D), D_FF ** -0.5),
        'final_norm_g': 1.0 + nrm(ks[28], (D,), 0.05),
    }


def reference(x_prompt, x_sample, c, c_ctx, cache_attn_k, cache_attn_v, state_mlstm_C, state_mlstm_n,
              state_mlstm_m, state_hgrn_S, ada_w, ada_b, norm_mix_g, norm_ffn_g, ev_w_in, ev_gate_b,
              ev_lb_logits, ml_norm_g, hg_norm_g, ev_w_out, od_w_in, od_lambda, da_norm_g, od_w_out,
              ffn_w1, ffn_w3, ffn_w2, final_norm_g):
    Bp = x_prompt.shape[0]
    zC = jnp.zeros((Bp, 2, ML_HEADS, ML_DK, ML_DV), f32)
    zn = jnp.zeros((Bp, 2, ML_HEADS, ML_DK), f32)
    zm = jnp.zeros((Bp, 2, ML_HEADS), f32)
    zS = jnp.zeros((Bp, 2, HG_HEADS, HG_DK, HG_DV), f32)
    xp = x_prompt.astype(f32)
    xs = x_sample.astype(f32)
    ks_, vs_, Cs_, ns_, ms_, Ss_ = [], [], [], [], [], []
    for l in range(DEPTH):
        sh_p, sc_p, g_p, sh2_p, sc2_p, g2_p = _modulation(c_ctx, ada_w[l], ada_b[l])
        sh_s, sc_s, g_s, sh2_s, sc2_s, g2_s = _modulation(c, ada_w[l], ada_b[l])
        hp = _rmsnorm(xp, norm_mix_g[l]) * (1.0 + sc_p) + sh_p
        hs = _rmsnorm(xs, norm_mix_g[l]) * (1.0 + sc_s) + sh_s
        if l % 2 == 0:
            e = l // 2
            mp, st = _even_mixer(hp, e, ev_w_in, ev_gate_b, ev_lb_logits, ml_norm_g, hg_norm_g, ev_w_out,
                                 zC, zn, zm, zS)
            ms, _ = _even_mixer(hs, e, ev_w_in, ev_gate_b, ev_lb_logits, ml_norm_g, hg_norm_g, ev_w_out,
                                state_mlstm_C[:, e], state_mlstm_n[:, e], state_mlstm_m[:, e], state_hgrn_S[:, e])
            Cs_.append(st[0])
            ns_.append(st[1])
            ms_.append(st[2])
            Ss_.append(st[3])
        else:
            o = l // 2
            mp, cache = _odd_mixer(hp, o, l, od_w_in, od_lambda, da_norm_g, od_w_out, None, None)
            ms, _ = _odd_mixer(hs, o, l, od_w_in, od_lambda, da_norm_g, od_w_out,
                               cache_attn_k[:, o], cache_attn_v[:, o])
            ks_.append(cache[0])
            vs_.append(cache[1])
        xp = xp + g_p * mp
        xs = xs + g_s * ms
        hp = _rmsnorm(xp, norm_ffn_g[l]) * (1.0 + sc2_p) + sh2_p
        hs = _rmsnorm(xs, norm_ffn_g[l]) * (1.0 + sc2_s) + sh2_s
        xp = xp + g2_p * _swiglu(hp, ffn_w1[l], ffn_w3[l], ffn_w2[l])
        xs = xs + g2_s * _swiglu(hs, ffn_w1[l], ffn_w3[l], ffn_w2[l])
    y_prompt = _rmsnorm(xp, final_norm_g)
    y_sample = _rmsnorm(xs, final_norm_g)
    new_attn_k = jnp.stack(ks_, axis=1)
    new_attn_v = jnp.stack(vs_, axis=1)
    new_mlstm_C = jnp.stack(Cs_, axis=1)
    new_mlstm_n = jnp.stack(ns_, axis=1)
    new_mlstm_m = jnp.stack(ms_, axis=1)
    new_hgrn_S = jnp.stack(Ss_, axis=1)
    return (y_prompt, y_sample, new_attn_k, new_attn_v, new_mlstm_C, new_mlstm_n, new_mlstm_m, new_hgrn_S)
```

```python
import math
from contextlib import ExitStack
import numpy as np
import ml_dtypes
import concourse.bass as bass
import concourse.mybir as mybir
from concourse.bass_utils import run_bass_kernel_spmd

F32 = mybir.dt.float32
BF16 = mybir.dt.bfloat16
AF = mybir.ActivationFunctionType
ALU = mybir.AluOpType
AX = mybir.AxisListType

D = 1024
KC = 8
DFF = 2816
FC = 22
EPS = 1e-6
NCORE = 8
PSEQ = 4
PS = 256
SS = 2048
PAST = 256
NQ = 512
EV_IN = 4624
OFF_MLQ, OFF_MLK, OFF_MLV, OFF_MLO, OFF_MLG = 0, 512, 1024, 1536, 2048
OFF_HGQ, OFF_HGFF, OFF_HGFB, OFF_HGI, OFF_HGO = 2064, 2576, 3088, 3600, 4112
LAM_INIT1 = 0.8 - 0.6 * math.exp(-0.3 * 1)

ENGS = ("pe", "act", "dve", "pool", "sp")


class Res:
    __slots__ = ("lw", "rd", "excl")

    def __init__(self, excl=False):
        self.lw = None
        self.rd = {}
        self.excl = excl


class Prog:
    def __init__(self, nc, es, ndma=8):
        self.nc = nc
        self.q = {e: [] for e in ENGS}
        self.cnt = {}
        self.seen = {e: {} for e in ENGS}
        self.sem = {}
        self.ndma = ndma
        self.rr = {e: 0 for e in ENGS}
        for e in ENGS:
            self.sem[("e", e)] = es.enter_context(nc.semaphore("s_" + e))
        for e in ("sp", "pool", "act"):
            for i in range(ndma):
                self.sem[("d", e, i)] = es.enter_context(nc.semaphore("d_%s%d" % (e, i)))
        self.out_tokens = []

    def _deps(self, eng, reads, writes):
        need = {}

        def add(tok):
            if tok is None:
                return
            k, v = tok
            if eng == "pe" and k == ("e", "pe"):
                return
            if need.get(k, 0) < v:
                need[k] = v

        for r in reads:
            add(r.lw)
            if r.excl:
                for k, v in r.rd.items():
                    if k != ("e", eng):
                        add((k, v))
        for w in writes:
            add(w.lw)
            for k, v in w.rd.items():
                add((k, v))
        out = []
        for k, v in need.items():
            if self.seen[eng].get(k, 0) >= v:
                continue
            self.seen[eng][k] = v
            out.append((k, v))
        return out

    def _commit(self, tok, reads, writes):
        for w in writes:
            w.lw = tok
            w.rd = {}
        for r in reads:
            if r in writes:
                continue
            k, v = tok
            if r.rd.get(k, 0) < v:
                r.rd[k] = v

    def op(self, eng, fn, reads=(), writes=()):
        waits = self._deps(eng, reads, writes)
        k = ("e", eng)
        self.cnt[k] = self.cnt.get(k, 0) + 1
        tok = (k, self.cnt[k])
        self.q[eng].append((waits, fn, k, 1))
        self._commit(tok, reads, writes)
        return tok

    def dma(self, eng, out, in_, reads=(), writes=(), is_output=False):
        i = self.rr[eng]
        self.rr[eng] = (i + 1) % self.ndma
        k = ("d", eng, i)
        prev = self.cnt.get(k, 0)
        waits = self._deps(eng, reads, writes)
        if prev > 0 and self.seen[eng].get(k, 0) < prev:
            self.seen[eng][k] = prev
            waits.append((k, prev))
        self.cnt[k] = prev + 16
        tok = (k, prev + 16)
        self.q[eng].append((waits, (lambda e: e.dma_start(out=out, in_=in_)), k, 16))
        self._commit(tok, reads, writes)
        if is_output:
            self.out_tokens.append(tok)
        return tok

    def fence(self, eng, res_wait, res_reset):
        waits = self._deps(eng, [], res_wait)
        if waits:
            self.q[eng].append((waits, None, None, 0))
        for r in list(res_wait) + list(res_reset):
            r.lw = None
            r.rd = {}

    def barrier(self):
        cur = dict(self.cnt)
        for e in ENGS:
            waits = []
            for k, v in cur.items():
                if k == ("e", e):
                    continue
                if self.seen[e].get(k, 0) >= v:
                    continue
                self.seen[e][k] = v
                waits.append((k, v))
            if waits:
                self.q[e].append((waits, None, None, 0))

    def finish(self):
        cur = dict(self.cnt)
        waits = [(k, v) for k, v in cur.items() if k != ("e", "sp")]
        self.q["sp"].append((waits, None, None, 0))
        nc = self.nc
        with nc.Block() as block:
            for eng, deco in (("pe", block.tensor), ("act", block.scalar), ("dve", block.vector),
                              ("pool", block.gpsimd), ("sp", block.sync)):
                def body(e, eng=eng):
                    for waits, fn, k, inc in self.q[eng]:
                        for wk, wv in waits:
                            e.wait_ge(self.sem[wk], wv)
                        if fn is not None:
                            fn(e).then_inc(self.sem[k], inc)
                deco(body)


STAGE = 4
_LAST = {}
WCACHE = True
SEQ_PASSES = False
DEBUG = False
SKIP_NM = False
ODD_LEVEL = 4
ODD_OUT = True


def build_program(dbg=()):
    nc = bass.Bass("TRN2", target_bir_lowering=False)
    es = ExitStack()
    P = Prog(nc, es)
    dbg_out = {}

    def din(name, shape, dt=F32):
        return nc.dram_tensor(name, list(shape), dt, kind="ExternalInput").ap()

    def dout(name, shape, dt=F32):
        return nc.dram_tensor(name, list(shape), dt, kind="ExternalOutput").ap()

    _uid = [0]

    def sb(stack, name, shape, dt=F32):
        _uid[0] += 1
        return stack.enter_context(nc.sbuf_tensor("%s_%d" % (name, _uid[0]), list(shape), dt))

    xpT = din("xpT", [D, PSEQ * PS])
    xsT = din("xsT", [D, SS])
    condT = din("condT", [128, KC, 2])
    ada_w = din("ada_w", [2, D, 6 * D])
    ada_bT = din("ada_bT", [128, 2, 48])
    nmixT = din("nmixT", [128, 2, KC])
    nffnT = din("nffnT", [128, 2, KC])
    finT = din("finT", [128, KC])
    ev_w_in = din("ev_w_in", [D, EV_IN])
    gate_b = din("gate_b", [128, 16])
    lb_log = din("lb_log", [128, 2, 512])
    mlg_g = din("mlg_g", [128, 512])
    hgg_g = din("hgg_g", [128, 512])
    ev_w_out = din("ev_w_out", [D, D])
    od_w_in = din("od_w_in", [D, 3 * D])
    od_lam = din("od_lam", [128, 4, 64])
    da_g = din("da_g", [128, 128])
    od_w_out = din("od_w_out", [D, D])
    ffn_w1 = din("ffn_w1", [2, D, DFF])
    ffn_w3 = din("ffn_w3", [2, D, DFF])
    ffn_w2 = din("ffn_w2", [2, DFF, D])
    ckT = din("ckT", [8, 128, PAST])
    cv = din("cv", [PAST, D])
    st_C = din("st_C", [2, 4, 128, 128])
    st_n = din("st_n", [8, 128])
    st_m = din("st_m", [8, 128])
    st_S = din("st_S", [2, 4, 128, 128])
    sel = din("sel", [128, 4])
    c_cosk = din("c_cosk", [128, SS])
    c_sink = din("c_sink", [128, SS])
    c_cosq = din("c_cosq", [128, NQ])
    c_sinq = din("c_sinq", [128, NQ])
    c_f32 = din("c_f32", [128, 7, 128])
    c_bf = din("c_bf", [128, 6, 128])
    c_rot = din("c_rot", [128, 128])

    dbgT = dout("dbgT", [D, PSEQ * PS], BF16) if DEBUG else None
    ypT = dout("ypT", [D, PSEQ * PS])
    ysT = dout("ysT", [D, NQ])
    o_k = dout("o_k", [PSEQ * PS, D])
    o_v = dout("o_v", [PSEQ * PS, D])
    o_C = dout("o_C", [PSEQ, 2, 4, 128, 128])
    o_n = dout("o_n", [PSEQ, 2, 4, 128])
    o_m = dout("o_m", [PSEQ, 2, 4])
    o_S = dout("o_S", [PSEQ, 2, 4, 128, 128])

    G = es
    cF = sb(G, "cF", [128, 7, 128], F32)
    cB = sb(G, "cB", [128, 6, 128], BF16)
    cRot = sb(G, "cRot", [128, 128], BF16)
    r_const = Res()
    P.dma("sp", cF[:], c_f32[:, :, :], writes=[r_const])
    P.dma("pool", cB[:], c_bf[:, :, :], writes=[r_const])
    P.dma("pool", cRot[:], c_rot[:, :], writes=[r_const])
    epsT = sb(G, "epsT", [128, 4], F32)
    P.op("dve", lambda e: e.memset(epsT[:, 0:1], EPS), [], [r_const])
    P.op("dve", lambda e: e.memset(epsT[:, 1:2], 1.0), [], [r_const])
    P.op("dve", lambda e: e.memset(epsT[:, 2:3], 0.0), [], [r_const])
    identF, triF, triB, onesF, suF, suB = (cF[:, i, :] for i in range(6))
    CI = cF[:, 6, 0:4]
    identB, onesB, maskF, maskB, m32F, m32B = (cB[:, i, :] for i in range(6))

    pbank = [es.enter_context(nc.psum_tensor("pb%d" % i, [128, 512], F32)) for i in range(7)]
    pres = [Res(True) for _ in range(7)]
    pbf = es.enter_context(nc.psum_tensor("pbf", [128, 1024], BF16))
    pbf_res = [Res(True)] * 2

    NWB = 3
    WEL = 4096
    wring = [sb(G, "wr%d" % i, [128, WEL], BF16) for i in range(NWB)]
    wres = [Res() for _ in range(NWB)]
    wctr = [0]

    NST = 2
    STEL = 2816
    wstage = [sb(G, "wst%d" % i, [128, STEL], F32) for i in range(NST)]
    wstres = [Res() for _ in range(NST)]
    stctr = [0]
    qctr = [0]
    cctr = [0]

    wcache = {}

    def wpart(dst_view, src_ap, k, n, dres):
        assert k * n <= STEL
        key = repr(src_ap)
        if WCACHE and key in wcache:
            cap, cres = wcache[key]
            _LAST["hits"] = _LAST.get("hits", 0) + 1
            q = ("sp", "act")[qctr[0] % 2]
            qctr[0] += 1
            P.dma(q, dst_view, cap.rearrange("p (k n) -> p k n", n=n), reads=[cres], writes=[dres])
            return
        i = stctr[0] % NST
        stctr[0] += 1
        st = wstage[i][:, 0:k * n].rearrange("p (k n) -> p k n", n=n)
        q = ("sp", "act")[qctr[0] % 2]
        qctr[0] += 1
        P.dma(q, st, src_ap.rearrange("(k p) n -> p k n", p=128), writes=[wstres[i]])
        ce = ("act", "dve", "pool", "dve", "act")[cctr[0] % 5]
        cctr[0] += 1
        if ce == "act":
            P.op("act", lambda e: e.activation(out=dst_view, in_=st, func=AF.Copy), [wstres[i]], [dres])
        else:
            P.op(ce, lambda e: e.tensor_copy(out=dst_view, in_=st), [wstres[i]], [dres])
        if WCACHE:
            cap = nc.dram_tensor("wc%d" % len(wcache), [128, k * n], BF16, kind="Internal").ap()
            cres = Res()
            wcache[key] = (cap, cres)
            q2 = ("act", "sp")[qctr[0] % 2]
            P.dma(q2, cap.rearrange("p (k n) -> p k n", n=n), dst_view, reads=[dres], writes=[cres])

    def wload(parts):
        i = wctr[0] % NWB
        wctr[0] += 1
        t, r = wring[i], wres[i]
        for src, kc, off, n in parts:
            dst = t[:, off:off + kc * n].rearrange("p (k n) -> p k n", n=n)
            nsub = 1
            while kc * (n // nsub) > STEL:
                nsub *= 2
            ns = n // nsub
            for s_ in range(nsub):
                wpart(dst[:, :, s_ * ns:(s_ + 1) * ns], src[:, s_ * ns:(s_ + 1) * ns], kc, ns, r)
        return t, r

    def wview(t, kc, n, off=0):
        return t[:, off:off + kc * n].rearrange("p (k n) -> p k n", n=n)

    modv = sb(G, "modv", [128, 2, 48, 2], F32)
    gm = sb(G, "gm", [128, 2, 2, 2, KC], F32)
    r_mod = Res()

    def phase_adaln():
        with ExitStack() as ph:
            cs = sb(ph, "cs", [128, KC, 2], F32)
            sg = sb(ph, "sg", [128, KC, 2], F32)
            csb = sb(ph, "csb", [128, KC, 2], BF16)
            abT = sb(ph, "abT", [128, 2, 48], F32)
            nm = sb(ph, "nm", [128, 2, KC], F32)
            nf = sb(ph, "nf", [128, 2, KC], F32)
            r_c = Res()
            P.dma("sp", cs[:], condT[:, :, :], writes=[r_c])
            P.dma("sp", abT[:], ada_bT[:, :, :], writes=[r_c])
            P.dma("sp", nm[:], nmixT[:, :, :], writes=[r_c])
            P.dma("sp", nf[:], nffnT[:, :, :], writes=[r_c])
            r_s = Res()
            P.op("act", lambda e: e.activation(out=sg[:], in_=cs[:], func=AF.Sigmoid), [r_c], [r_s])
            csf = sb(ph, "csf", [128, KC, 2], F32)
            P.op("dve", lambda e: e.tensor_tensor(out=csf[:], in0=cs[:], in1=sg[:], op=ALU.mult), [r_c, r_s], [r_s])

            def MM0(out, lhsT, rhs, start, stop, reads, writes):
                P.op("pe", lambda e: e.matmul(out, lhsT=lhsT, rhs=rhs, start=start, stop=stop), reads, writes)

            def TS0(out, in0, s1, reads, writes):
                P.op("dve", lambda e: e.tensor_scalar(out=out, in0=in0, scalar1=s1, scalar2=None, op0=ALU.add),
                     reads, writes)
            for l in range(2):
                for cb in range(24):
                    i = stctr[0] % NST
                    stctr[0] += 1
                    st = wstage[i][:, 0:KC * 256].rearrange("p (k n) -> p k n", n=256)
                    q = ("sp", "act")[qctr[0] % 2]
                    qctr[0] += 1
                    P.dma(q, st, ada_w[l, :, cb * 256:(cb + 1) * 256].rearrange("(k p) n -> p k n", p=128),
                          writes=[wstres[i]])
                    pb, pr = pbank[cb % 2], pres[cb % 2]
                    for m in range(2):
                        for kc in range(KC):
                            MM0(pb[:, 2 * m:2 * m + 2], st[:, kc, m * 128:(m + 1) * 128], csf[:, kc, :],
                                kc == 0, kc == KC - 1, [wstres[i], r_s], [pr])
                    for m in range(2):
                        ecol = cb * 2 + m
                        TS0(modv[:, l, ecol, :], pb[:, 2 * m:2 * m + 2], abT[:, l, ecol:ecol + 1], [pr, r_c], [r_mod])
                for mi, (wh, ng) in enumerate(((1, nm), (4, nf))):
                    for w in range(2):
                        P.op("dve", lambda e, l=l, mi=mi, wh=wh, ng=ng, w=w: e.scalar_tensor_tensor(
                            out=gm[:, l, mi, w, :], in0=modv[:, l, wh * 8:(wh + 1) * 8, w], scalar=1.0,
                            in1=ng[:, l, :], op0=ALU.add, op1=ALU.mult), [r_mod, r_c], [r_mod])
        P.barrier()

    def mvec(l, which, w):
        return modv[:, l, which * 8:(which + 1) * 8, w]

    def norm_mod(ph_scr, xT, xres, t0, n, gmv, shv, outT, ot0, outres, extra_reads=()):
        sq, rstd, tmp, r_sq, r_rstd, r_tmp = ph_scr
        for kc in range(KC):
            eng = "act" if kc % 2 == 0 else "pool"
            if eng == "act":
                P.op("act", lambda e, kc=kc: e.activation(out=sq[:, kc, :n], in_=xT[:, kc, t0:t0 + n], func=AF.Square),
                     [xres], [r_sq[kc]])
            else:
                P.op("pool", lambda e, kc=kc: e.tensor_tensor(out=sq[:, kc, :n], in0=xT[:, kc, t0:t0 + n],
                                                              in1=xT[:, kc, t0:t0 + n], op=ALU.mult), [xres], [r_sq[kc]])
        pb, pr = pbank[6], pres[6]
        for kc in range(KC):
            P.op("pe", lambda e, kc=kc: e.matmul(pb[:, :n], lhsT=onesB, rhs=sq[:, kc, :n], start=(kc == 0),
                                                 stop=(kc == KC - 1)), [r_sq[kc], r_const], [pr])
        P.op("act", lambda e: e.activation(out=rstd[:, :n], in_=pb[:, :n], func=AF.Ln, scale=1.0 / D,
                                           bias=epsT[:, 0:1]), [pr, r_const], [r_rstd])
        P.op("act", lambda e: e.activation(out=rstd[:, :n], in_=rstd[:, :n], func=AF.Exp, scale=-0.5),
             [r_rstd], [r_rstd])
        for kc in range(KC):
            P.op("dve", lambda e, kc=kc: e.scalar_tensor_tensor(
                out=tmp[:, kc % 2, :n], in0=xT[:, kc, t0:t0 + n], scalar=gmv[:, kc:kc + 1], in1=rstd[:, :n],
                op0=ALU.mult, op1=ALU.mult), [xres, r_rstd, r_mod] + list(extra_reads), [r_tmp[kc % 2]])
            if shv is not None:
                P.op("act", lambda e, kc=kc: e.activation(out=outT[:, kc, ot0:ot0 + n], in_=tmp[:, kc % 2, :n],
                                                          func=AF.Identity, bias=shv[:, kc:kc + 1], scale=1.0),
                     [r_tmp[kc % 2], r_mod], [outres])
            else:
                P.op("act", lambda e, kc=kc: e.activation(out=outT[:, kc, ot0:ot0 + n], in_=tmp[:, kc % 2, :n],
                                                          func=AF.Copy, scale=1.0), [r_tmp[kc % 2]], [outres])

    def mk_norm_scr(ph):
        sq = sb(ph, "n_sq", [128, KC, 512], BF16)
        rstd = sb(ph, "n_rstd", [128, 512], F32)
        tmp = sb(ph, "n_tmp", [128, 2, 512], F32)
        return (sq, rstd, tmp, [Res() for _ in range(KC)], Res(), [Res(), Res()])

    mmrr = [0]

    def next_bank():
        i = mmrr[0] % 6
        mmrr[0] += 1
        return pbank[i], pres[i]

    def linear_fm(w_ap, kcn, ncols, inT, in_res, T, epilogue, col0=0):
        nblk = (ncols + 511) // 512
        per = WEL // (kcn * 128)
        per = min(per, 4)
        blocks = []
        c = 0
        while c < ncols:
            n = min(per * 128, ncols - c)
            blocks.append((c, n))
            c += n
        def mk(b):
            c, n = b
            return [(w_ap[:, col0 + c:col0 + c + n], kcn, 0, n)]
        loaded = [wload(mk(blocks[0]))]
        for bi, (c, n) in enumerate(blocks):
            if bi + 1 < len(blocks):
                loaded.append(wload(mk(blocks[bi + 1])))
            wt, wr = loaded[bi]
            wv = wview(wt, kcn, n)
            for t0 in range(0, T, 512):
                tn = min(512, T - t0)
                for m in range(n // 128):
                    pb, pr = next_bank()
                    for kc in range(kcn):
                        P.op("pe", lambda e, pb=pb, wv=wv, m=m, kc=kc, t0=t0, tn=tn: e.matmul(
                            pb[:, :tn], lhsT=wv[:, kc, m * 128:(m + 1) * 128], rhs=inT[:, kc, t0:t0 + tn],
                            start=(kc == 0), stop=(kc == kcn - 1)), [wr, in_res], [pr])
                    epilogue((c // 128) + m, t0, tn, pb, pr)

    def residual_epilogue(xT, xres, gv, xoff=0):
        def ep(mc, t0, tn, pb, pr):
            P.op("dve", lambda e: e.scalar_tensor_tensor(
                out=xT[:, mc, xoff + t0:xoff + t0 + tn], in0=pb[:, :tn], scalar=gv[:, mc:mc + 1],
                in1=xT[:, mc, xoff + t0:xoff + t0 + tn], op0=ALU.mult, op1=ALU.add), [pr, r_mod, xres], [xres])
        return ep

    def ffn(ph, l, hT, hres, T, xT, xres, gv, TILE=512):
        TILE = min(TILE, T)
        hid = sb(ph, "hid", [128, FC, TILE], BF16)
        sil = sb(ph, "sil", [128, 2, 512], F32)
        r_hid = Res()
        r_sil = [Res(), Res()]
        sctr = [0]

        def emitA(pa, pra, pb_, prb, w1v, w3v, wr, m, fc, ht0, hn, off):
            for kc in range(KC):
                MM(pa[:, :hn], w1v[:, kc, m * 128:(m + 1) * 128], hT[:, kc, ht0:ht0 + hn], kc == 0, kc == KC - 1,
                   [wr, hres], [pra])
            for kc in range(KC):
                MM(pb_[:, :hn], w3v[:, kc, m * 128:(m + 1) * 128], hT[:, kc, ht0:ht0 + hn], kc == 0, kc == KC - 1,
                   [wr, hres], [prb])
            si = sctr[0] % 2
            sctr[0] += 1
            ACT(sil[:, si, :hn], pa[:, :hn], AF.Silu, [pra], [r_sil[si]])
            TT("dve", hid[:, fc, off:off + hn], sil[:, si, :hn], pb_[:, :hn], ALU.mult, [prb, r_sil[si]], [r_hid])

        def emitB(pb, pr, wv, wr, m, mc, ht0, hn, off):
            for fc in range(FC):
                MM(pb[:, :hn], wv[:, fc, m * 128:(m + 1) * 128], hid[:, fc, off:off + hn], fc == 0, fc == FC - 1,
                   [wr, r_hid], [pr])
            STT(xT[:, mc, ht0:ht0 + hn], pb[:, :hn], gv[:, mc:mc + 1], xT[:, mc, ht0:ht0 + hn], ALU.mult, ALU.add,
                [pr, r_mod, xres], [xres])

        for t0 in range(0, T, TILE):
            tn_all = min(TILE, T - t0)
            halves = [(t0 + o, min(512, tn_all - o), o) for o in range(0, tn_all, 512)]
            blocks = [(c, min(256, DFF - c)) for c in range(0, DFF, 256)]

            def mk(b_):
                c, n = b_
                return [(ffn_w1[l, :, c:c + n], KC, 0, n), (ffn_w3[l, :, c:c + n], KC, KC * 256, n)]
            loaded = [wload(mk(blocks[0]))]
            for bi, (c, n) in enumerate(blocks):
                if bi + 1 < len(blocks):
                    loaded.append(wload(mk(blocks[bi + 1])))
                wt, wr = loaded[bi]
                w1v = wview(wt, KC, n, 0)
                w3v = wview(wt, KC, n, KC * 256)
                for m in range(n // 128):
                    for ht0, hn, off in halves:
                        pa, pra = next_bank()
                        pb_, prb = next_bank()
                        emitA(pa, pra, pb_, prb, w1v, w3v, wr, m, c // 128 + m, ht0, hn, off)
            blocks = [(c, 128) for c in range(0, D, 128)]

            def mk2(b_):
                c, n = b_
                return [(ffn_w2[l, :, c:c + n], FC, 0, n)]
            loaded = [wload(mk2(blocks[0]))]
            for bi, (c, n) in enumerate(blocks):
                if bi + 1 < len(blocks):
                    loaded.append(wload(mk2(blocks[bi + 1])))
                wt, wr = loaded[bi]
                wv = wview(wt, FC, n)
                for m in range(n // 128):
                    for ht0, hn, off in halves:
                        pb, pr = next_bank()
                        emitB(pb, pr, wv, wr, m, c // 128 + m, ht0, hn, off)

    def MM(out, lhsT, rhs, start, stop, reads, writes):
        P.op("pe", lambda e: e.matmul(out, lhsT=lhsT, rhs=rhs, start=start, stop=stop), reads, writes)

    def TR(out, in_, ident, reads, writes):
        P.op("pe", lambda e: e.transpose(out, in_, ident), reads, writes)

    def ACT(out, in_, func, reads, writes, bias=None, scale=None, accum=None):
        kw = {}
        if bias is not None:
            kw["bias"] = bias
        if scale is not None:
            kw["scale"] = scale
        if accum is not None:
            kw["accum_out"] = accum
        P.op("act", lambda e: e.activation(out=out, in_=in_, func=func, **kw), reads, writes)

    def TT(eng, out, in0, in1, op, reads, writes):
        P.op(eng, lambda e: e.tensor_tensor(out=out, in0=in0, in1=in1, op=op), reads, writes)

    def TS(eng, out, in0, s1, op0, reads, writes, s2=None, op1=None):
        if op1 is None:
            P.op(eng, lambda e: e.tensor_scalar(out=out, in0=in0, scalar1=s1, scalar2=None, op0=op0), reads, writes)
        else:
            P.op(eng, lambda e: e.tensor_scalar(out=out, in0=in0, scalar1=s1, scalar2=s2, op0=op0, op1=op1),
                 reads, writes)

    def STT(out, in0, scalar, in1, op0, op1, reads, writes):
        P.op("dve", lambda e: e.scalar_tensor_tensor(out=out, in0=in0, scalar=scalar, in1=in1, op0=op0, op1=op1),
             reads, writes)

    def CP(eng, out, in_, reads, writes):
        if eng == "act":
            P.op("act", lambda e: e.activation(out=out, in_=in_, func=AF.Copy), reads, writes)
        else:
            P.op(eng, lambda e: e.tensor_copy(out=out, in_=in_), reads, writes)

    def MSET(eng, ap, val, writes):
        P.op(eng, lambda e: e.memset(ap, val), [], writes)

    def RECIP(out, in_, reads, writes):
        P.op("dve", lambda e: e.reciprocal(out=out, in_=in_), reads, writes)

    def RMAX(out, in_, reads, writes):
        P.op("dve", lambda e: e.reduce_max(out=out, in_=in_, axis=AX.X), reads, writes)

    def RSUM(out, in_, reads, writes):
        P.op("dve", lambda e: e.reduce_sum(out=out, in_=in_, axis=AX.X), reads, writes)

    ONE = epsT[:, 1:2]
    EPSC = epsT[:, 0:1]

    def round_robin(gens):
        gens = list(gens)
        while gens:
            for g in list(gens):
                try:
                    next(g)
                except StopIteration:
                    gens.remove(g)

    lbT = sb(G, "lbT", [128, 512], F32)
    omlT = sb(G, "omlT", [128, 512], F32)
    mlgT = sb(G, "mlgT", [128, 512], F32)
    hggT = sb(G, "hggT", [128, 512], F32)
    gbT = sb(G, "gbT", [128, 16], F32)
    dagS = sb(G, "dagS", [128, 128], F32)
    lamv = sb(G, "lamv", [128, 4], F32)
    r_par = Res()

    def phase_params():
        with phase() as ph:
            lbl = sb(ph, "lbl", [128, 2, 512], F32)
            lml = sb(ph, "lml", [128, 4, 64], F32)
            tmp = sb(ph, "ptmp", [128, 2, 64], F32)
            r_l = Res()
            P.dma("sp", lbl[:], lb_log[:, :, :], writes=[r_l])
            P.dma("sp", lml[:], od_lam[:, :, :], writes=[r_l])
            P.dma("sp", mlgT[:], mlg_g[:, :], writes=[r_par])
            P.dma("sp", hggT[:], hgg_g[:, :], writes=[r_par])
            P.dma("sp", gbT[:], gate_b[:, :], writes=[r_par])
            P.dma("sp", dagS[:], da_g[:, :], writes=[r_par])
            TT("dve", lbT[:], lbl[:, 0, :], lbl[:, 1, :], ALU.subtract, [r_l], [r_par])
            ACT(lbT[:], lbT[:], AF.Sigmoid, [r_par], [r_par])
            TS("dve", omlT[:], lbT[:], -1.0, ALU.mult, [r_par], [r_par], s2=1.0, op1=ALU.add)
            TS("dve", dagS[:], dagS[:], float(1.0 - LAM_INIT1), ALU.mult, [r_par], [r_par])
            TT("dve", tmp[:, 0, :], lml[:, 0, :], lml[:, 1, :], ALU.mult, [r_l], [r_l])
            TT("dve", tmp[:, 1, :], lml[:, 2, :], lml[:, 3, :], ALU.mult, [r_l], [r_l])
            RSUM(lamv[:, 2:3], tmp[:, 0, :], [r_l], [r_par])
            RSUM(lamv[:, 3:4], tmp[:, 1, :], [r_l], [r_par])
            ACT(lamv[:, 2:4], lamv[:, 2:4], AF.Exp, [r_par], [r_par])
            TT("dve", lamv[:, 0:1], lamv[:, 2:3], lamv[:, 3:4], ALU.subtract, [r_par], [r_par])
            TS("dve", lamv[:, 0:1], lamv[:, 0:1], float(LAM_INIT1), ALU.add, [r_par], [r_par])
            TS("dve", lamv[:, 1:2], lamv[:, 0:1], -1.0, ALU.mult, [r_par], [r_par])

    SM = pbank[6]
    SMr = pres[6]

    def head_norm_out(scr, src_ap, gain_ap, gate_ap, dst_ap, reads, dres):
        junk, ss, t1, t2, r_s = scr
        ACT(junk[:], src_ap, AF.Square, reads, [r_s], accum=ss[:, 0:1])
        ACT(ss[:, 1:2], ss[:, 0:1], AF.Ln, [r_s, r_const], [r_s], scale=1.0 / 128, bias=EPSC)
        ACT(ss[:, 1:2], ss[:, 1:2], AF.Exp, [r_s], [r_s], scale=-0.5)
        STT(t1[:], src_ap, ss[:, 1:2], gain_ap, ALU.mult, ALU.mult, reads + [r_s, r_par], [r_s])
        if gate_ap is not None:
            TT("pool", t2[:], t1[:], gate_ap, ALU.mult, reads + [r_s], [r_s])
        else:
            CP("pool", t2[:], t1[:], [r_s], [r_s])
        TR(pbf[:, 0:128], t2[:], identB, [r_s, r_const], [pbf_res[0]])
        CP("act", dst_ap, pbf[:, 0:128], [pbf_res[0]], [dres])

    HNG = 8

    def mk_hnb(st):
        return {"junk": sb(st, "hb_j", [128, 128], F32), "ss": sb(st, "hb_ss", [128, 2 * HNG], F32),
                "t1": sb(st, "hb_t1", [128, HNG, 128], F32), "t2": sb(st, "hb_t2", [128, HNG, 128], BF16),
                "rj": Res(), "rss": Res(), "r1": [Res() for _ in range(HNG)], "r2": [Res() for _ in range(HNG)]}

    def head_norm_batch(hb, items):
        for g0 in range(0, len(items), HNG):
            grp = items[g0:g0 + HNG]
            n = len(grp)
            ss = hb["ss"]
            for k, (src_ap, gain_ap, gate_ap, dst_ap, reads, dres) in enumerate(grp):
                ACT(hb["junk"][:], src_ap, AF.Square, reads, [hb["rj"], hb["rss"]], accum=ss[:, k:k + 1])
            ACT(ss[:, HNG:HNG + n], ss[:, 0:n], AF.Ln, [hb["rss"], r_const], [hb["rss"]], scale=1.0 / 128, bias=EPSC)
            ACT(ss[:, HNG:HNG + n], ss[:, HNG:HNG + n], AF.Exp, [hb["rss"]], [hb["rss"]], scale=-0.5)
            for k, (src_ap, gain_ap, gate_ap, dst_ap, reads, dres) in enumerate(grp):
                STT(hb["t1"][:, k, :], src_ap, ss[:, HNG + k:HNG + k + 1], gain_ap, ALU.mult, ALU.mult,
                    reads + [hb["rss"], r_par], [hb["r1"][k]])
            for k, (src_ap, gain_ap, gate_ap, dst_ap, reads, dres) in enumerate(grp):
                if gate_ap is not None:
                    TT("pool", hb["t2"][:, k, :], hb["t1"][:, k, :], gate_ap, ALU.mult, reads + [hb["r1"][k]], [hb["r2"][k]])
                else:
                    CP("pool", hb["t2"][:, k, :], hb["t1"][:, k, :], [hb["r1"][k]], [hb["r2"][k]])
            for k in range(n):
                TR(pbf[:, k * 128:(k + 1) * 128], hb["t2"][:, k, :], identB, [hb["r2"][k], r_const], [pbf_res[0]])
            for k, (src_ap, gain_ap, gate_ap, dst_ap, reads, dres) in enumerate(grp):
                CP("act" if k % 2 == 0 else "dve", dst_ap, pbf[:, k * 128:(k + 1) * 128], [pbf_res[0]], [dres])

    def mk_hn_scr(st):
        return (sb(st, "hn_j", [128, 128], F32), sb(st, "hn_ss", [128, 2], F32), sb(st, "hn_t1", [128, 128], F32),
                sb(st, "hn_t2", [128, 128], BF16), Res())

    def even_mixer(seg, hT, hres, moT, mores, nseq, S, is_prompt, SP=None):
        SP = SP or S
        NT = S // 128
        NTp = SP // 128
        nparts = S // SP
        with phase() as ph:
            nsets = 1 if nparts == 1 else 2
            PB = []
            for s_ in range(nsets):
                bs = {}
                bs["qTm"] = sb(ph, "qTm", [128, SP], BF16)
                bs["kTm"] = sb(ph, "kTm", [128, SP], BF16)
                bs["qTh"] = sb(ph, "qTh", [128, SP], BF16)
                bs["ktm"] = sb(ph, "ktm", [128, NTp, 128], BF16)
                bs["v1"] = sb(ph, "v1", [128, NTp, 130], BF16)
                bs["g16"] = sb(ph, "g16", [128, NTp, 16], F32)
                bs["lp"] = sb(ph, "lp", [128, NTp, 8], F32)
                bs["Bt"] = sb(ph, "Bt", [128, NTp, 4], F32)
                bs["at"] = sb(ph, "at", [128, NTp, 2], F32)
                bs["amx"] = sb(ph, "amx", [128, NTp, 2], F32)
                bs["lfall"] = sb(ph, "lfall", [128, NTp, 2, 128], F32)
                bs["kcall"] = sb(ph, "kcall", [128, NTp, 2, 128], BF16)
                bs["itm"] = sb(ph, "itm", [128, NTp, 128], BF16)
                bs["r_proj"], bs["r_gate"], bs["r_v1"], bs["r_lf"] = Res(), Res(), Res(), Res()
                for i in range(NTp):
                    MSET("pool", bs["v1"][:, i, 128:130], 1.0, [bs["r_v1"]])
                PB.append(bs)
            sgo = sb(ph, "sgo", [128, NT, 128], BF16)
            ohg = sb(ph, "ohg", [128, NT, 128], BF16)
            hsum = sb(ph, "hsum", [128, NT, 128], F32)
            osum = sb(ph, "osum", [128, NT, 128], F32)
            scrA = sb(ph, "scrA", [128, 256], F32)
            e1 = sb(ph, "e1", [128, 8, 8], F32)
            sm2 = sb(ph, "sm2", [128, 24], F32)
            hn = mk_hnb(ph)
            r_scr, r_go = Res(), Res()
            r_hs = [Res() for _ in range(NT)]
            r_os = [Res() for _ in range(NT)]
            ch = []
            for c in range(4):
                d = {}
                d["st"] = sb(ph, "st%d" % c, [128, 130], F32)
                d["m"] = sb(ph, "m%d" % c, [128, 8], F32)
                d["b1"] = sb(ph, "b1%d" % c, [128, 130], BF16)
                d["b2"] = sb(ph, "b2%d" % c, [128, 130], BF16)
                d["b3"] = sb(ph, "b3%d" % c, [128, 128], BF16)
                d["b4"] = sb(ph, "b4%d" % c, [128, 128], BF16)
                d["f1"] = sb(ph, "f1%d" % c, [128, 128], F32)
                d["f2"] = sb(ph, "f2%d" % c, [128, 128], F32)
                d["e4"] = sb(ph, "e4%d" % c, [128, 4], F32)
                d["q4"] = sb(ph, "q4%d" % c, [128, 4, 128], BF16)
                d["k4"] = sb(ph, "k4%d" % c, [128, 4, 128], BF16)
                d["b5"] = sb(ph, "b5%d" % c, [128, 2, 128], BF16)
                d["r"] = Res()
                d["rs"] = Res()
                d["R"] = {k: Res() for k in ("m0", "G", "nG", "u", "wi", "fl", "dn", "rc", "b1", "b2", "b3", "b4", "f1", "f2",
                                             "e4", "q4", "k4", "st", "sb0", "sb1")}
                ch.append(d)
            pbh = [pbf[:, 128:256], pbf[:, 256:384]]
            pbhr = [pbf_res[0], pbf_res[0]]
            MLP = []
            HGP = []
            for d_ in range(2):
                bk = pbank[d_]
                rb = Res(True)
                MLP.append({"ST": bk[:, 0:128], "ND": bk[:, 128:257], "dCN": bk[:, 257:386],
                            "rST": rb, "rND": rb, "rdCN": rb})
                ba, bb = pbank[2 + 2 * d_], pbank[3 + 2 * d_]
                ra, rbb = Res(True), Res(True)
                HGP.append({"D1": ba[:, 0:128], "D1T": ba[:, 128:256], "sc": ba[:, 256:384], "O": ba[:, 384:512],
                            "dS": [bb[:, 0:128], bb[:, 128:256]], "al": bb[:, 256:260],
                            "rD1": ra, "rD1T": ra, "rsc": ra, "rO": ra,
                            "rdS": [rbb, rbb], "ral": rbb})

            def chain_bank_fence(after):
                regs = []
                for d_ in range(2):
                    regs += [MLP[d_]["rST"], MLP[d_]["rND"], MLP[d_]["rdCN"], HGP[d_]["rD1"], HGP[d_]["rD1T"], HGP[d_]["rsc"],
                             HGP[d_]["rO"], HGP[d_]["rdS"][0], HGP[d_]["rdS"][1], HGP[d_]["ral"]]
                banks = [pres[i] for i in range(6)]
                if after:
                    P.fence("pe", regs, banks)
                else:
                    P.fence("pe", banks, regs)

            wsets = [[(wring[i], wres[i]) for i in range(3)]]
            if is_prompt:
                wsets.append([(sb(ph, "wx%d" % i, [128, WEL], BF16), Res()) for i in range(3)])

            def load_hg(h):
                (wf, wfr), (wa, war), (wb, wbr) = wsets[h % len(wsets)]
                wfv = wview(wf, KC, 384)
                wav = wview(wa, KC, 400)
                wbv = wview(wb, KC, 512)

                def wl(dst, off, n, wr_):
                    wpart(dst, ev_w_in[:, off:off + n], KC, n, wr_)
                wl(wfv[:, :, 0:128], OFF_MLQ + h * 128, 128, wfr)
                wl(wfv[:, :, 128:256], OFF_MLK + h * 128, 128, wfr)
                wl(wfv[:, :, 256:384], OFF_HGQ + h * 128, 128, wfr)
                wl(wav[:, :, 0:128], OFF_MLK + h * 128, 128, war)
                wl(wav[:, :, 128:256], OFF_MLV + h * 128, 128, war)
                wl(wav[:, :, 256:384], OFF_MLO + h * 128, 128, war)
                wl(wav[:, :, 384:400], OFF_MLG, 16, war)
                wl(wbv[:, :, 0:128], OFF_HGFF + h * 128, 128, wbr)
                wl(wbv[:, :, 128:256], OFF_HGFB + h * 128, 128, wbr)
                wl(wbv[:, :, 256:384], OFF_HGI + h * 128, 128, wbr)
                wl(wbv[:, :, 384:512], OFF_HGO + h * 128, 128, wbr)
                return wfv, wav, wbv, wfr, war, wbr

            wctr[0] += 3
            loaded_hg = {0: load_hg(0)}
            for h in range(4):
                if h not in loaded_hg:
                    loaded_hg[h] = load_hg(h)
                wfv, wav, wbv, wfr, war, wbr = loaded_hg.pop(h)
                if len(wsets) > 1 and h + 1 < 4:
                    loaded_hg[h + 1] = load_hg(h + 1)
                hs = slice(h * 128, (h + 1) * 128)

                def inproj(T0, gt0, bs, h=h, hs=hs, wfv=wfv, wav=wav, wbv=wbv, wfr=wfr, war=war, wbr=wbr):
                    qTm, kTm, qTh, ktm, v1, g16, lp, Bt, at, amx, lfall, kcall, itm = (bs[k] for k in (
                        "qTm", "kTm", "qTh", "ktm", "v1", "g16", "lp", "Bt", "at", "amx", "lfall", "kcall", "itm"))
                    r_proj, r_gate, r_v1 = bs["r_proj"], bs["r_gate"], bs["r_v1"]
                    r_lf = bs["r_lf"]
                    for j, (dst, scl) in enumerate(((qTm, 128 ** -0.5), (kTm, 1.0), (qTh, 1.0))):
                        for tb in range(0, SP, 512):
                            n = min(512, SP - tb)
                            pb, pr = next_bank()
                            for kc in range(KC):
                                MM(pb[:, :n], wfv[:, kc, j * 128:(j + 1) * 128], hT[:, kc, T0 + tb:T0 + tb + n],
                                   kc == 0, kc == KC - 1, [wfr, hres], [pr])
                            ACT(dst[:, tb:tb + n], pb[:, :n], AF.Copy, [pr], [r_proj], scale=float(scl))
                    for i in range(NTp):
                        tk = slice(T0 + i * 128, T0 + (i + 1) * 128)
                        pb, pr = next_bank()
                        for kc in range(KC):
                            MM(pb[:, 0:400], hT[:, kc, tk], wav[:, kc, 0:400], kc == 0, kc == KC - 1, [war, hres], [pr])
                        CP("dve", ktm[:, i, :], pb[:, 0:128], [pr], [r_proj])
                        CP("dve", v1[:, i, 0:128], pb[:, 128:256], [pr], [r_proj, r_v1])
                        CP("act", sgo[:, gt0 + i, :], pb[:, 256:384], [pr], [r_go])
                        TT("dve", g16[:, i, :], pb[:, 384:400], gbT[:], ALU.add, [pr, r_par], [r_gate])
                        pb, pr = next_bank()
                        for kc in range(KC):
                            MM(pb[:, 0:512], hT[:, kc, tk], wbv[:, kc, 0:512], kc == 0, kc == KC - 1, [wbr, hres], [pr])
                        CP("act", lfall[:, i, :, :], pb[:, 0:256].rearrange("p (a b) -> p a b", a=2), [pr], [r_lf])
                        CP("dve", itm[:, i, :], pb[:, 256:384], [pr], [r_proj])
                        CP("act", ohg[:, gt0 + i, :], pb[:, 384:512], [pr], [r_go])
                    ACT(sgo[:, gt0:gt0 + NTp, :], sgo[:, gt0:gt0 + NTp, :], AF.Sigmoid, [r_go], [r_go])
                    ACT(lfall[:, :, :, :], lfall[:, :, :, :], AF.Sigmoid, [r_lf], [r_lf])
                    for i in range(NTp):
                        for dd in range(2):
                            TT("dve", lfall[:, i, dd, :], lfall[:, i, dd, :], omlT[:, hs], ALU.mult, [r_lf, r_par], [r_lf])
                            TT("dve", lfall[:, i, dd, :], lfall[:, i, dd, :], lbT[:, hs], ALU.add, [r_lf, r_par], [r_lf])
                    TS("dve", kcall[:, :, :, :], lfall[:, :, :, :], -1.0, ALU.mult, [r_lf], [r_proj], s2=1.0, op1=ALU.add)
                    ACT(lfall[:, :, :, :], lfall[:, :, :, :], AF.Ln, [r_lf, r_proj], [r_lf, r_proj])
                    ACT(ohg[:, gt0:gt0 + NTp, :], ohg[:, gt0:gt0 + NTp, :], AF.Silu, [r_go], [r_go])
                    ACT(e1[:, 0:NTp, :], g16[:, :, 8:16], AF.Exp, [r_gate], [r_scr], scale=-1.0)
                    ACT(lp[:, :, :], e1[:, 0:NTp, :], AF.Ln, [r_scr, r_const], [r_gate], bias=ONE)
                    MM(SM[:, 0:NTp], triF, lp[:, :, h], True, True, [r_gate, r_const], [SMr])
                    MM(SM[:, NTp:2 * NTp], triB, lp[:, :, 4 + h], True, True, [r_gate, r_const], [SMr])
                    MM(SM[:, 2 * NTp:3 * NTp], onesF, lp[:, :, h], True, True, [r_gate, r_const], [SMr])
                    MM(SM[:, 3 * NTp:4 * NTp], onesF, lp[:, :, 4 + h], True, True, [r_gate, r_const], [SMr])
                    CP("dve", Bt[:, :, :], SM[:, 0:4 * NTp].rearrange("p (b a) -> p a b", b=4), [SMr], [r_gate])
                    TT("dve", at[:, :, 0:1], g16[:, :, h:h + 1], Bt[:, :, 0:1], ALU.add, [r_gate], [r_gate])
                    TT("dve", at[:, :, 1:2], g16[:, :, 4 + h:5 + h], Bt[:, :, 1:2], ALU.add, [r_gate], [r_gate])
                    n2 = 2 * NTp
                    TR(SM[0:n2, 128:256], at[:, :, :].rearrange("p a b -> p (a b)"), identF, [r_gate, r_const], [SMr])
                    RMAX(sm2[0:n2, 0:1], SM[0:n2, 128:256], [SMr], [r_scr])
                    TS("dve", sm2[0:n2, 4:4 + n2], identF[0:n2, 0:n2], sm2[0:n2, 0:1], ALU.mult, [r_scr, r_const], [r_scr])
                    MM(SM[:, 64:64 + n2], onesF[0:n2, :], sm2[0:n2, 4:4 + n2], True, True, [r_scr, r_const], [SMr])
                    CP("dve", amx[:, :, :], SM[:, 64:64 + n2].rearrange("p (a b) -> p a b", b=2), [SMr], [r_gate])

                def ml_init(d, c, h=h):
                    cn, mm_ = c["st"], c["m"]
                    R = c["R"]
                    allm = [R[k] for k in ("m0", "G", "nG", "u", "wi", "fl", "dn", "rc")]
                    if is_prompt:
                        MSET("pool", cn[:], 0.0, [R["st"]])
                        MSET("pool", mm_[:], 0.0, allm)
                    else:
                        MSET("pool", mm_[:], 0.0, allm)
                        P.dma("sp", cn[:, 0:128], st_C[d, h, :, :], writes=[R["st"]])
                        P.dma("sp", cn[:, 128:129], st_n[d * 4 + h, :].rearrange("(p o) -> p o", o=1), writes=[R["st"]])
                        P.dma("sp", mm_[:, 0:1], st_m[d * 4 + h, :].rearrange("(p o) -> p o", o=1), writes=[R["m0"]])

                def ml_final(seq, d, c, h=h):
                    cn, mm_ = c["st"], c["m"]
                    P.dma("sp", o_C[seq, d, h, :, :], cn[:, 0:128], reads=[c["R"]["st"]], is_output=True)
                    P.dma("sp", o_n[seq, d, h, :].rearrange("(p o) -> p o", o=1), cn[:, 128:129], reads=[c["R"]["st"]],
                          is_output=True)
                    P.dma("sp", o_m[seq, d, h:h + 1].rearrange("(p o) -> p o", o=1), mm_[0:1, 0:1], reads=[c["R"]["m0"]],
                          is_output=True)

                def ml_run(d, c, tiles, bs):
                    cn, mm_ = c["st"], c["m"]
                    R = c["R"]
                    qTm, kTm, ktm, v1, Bt, at, amx = (bs[k] for k in ("qTm", "kTm", "ktm", "v1", "Bt", "at", "amx"))
                    r_proj, r_gate, r_v1 = bs["r_proj"], bs["r_gate"], bs["r_v1"]
                    mp = MLP[d]
                    mask = maskF if d == 0 else maskB
                    yield
                    for i, ig in tiles:
                        tk = slice(i * 128, (i + 1) * 128)
                        m0, Gt, nG, u, wi, fl, dn, rc = (mm_[:, k:k + 1] for k in range(8))
                        MM(mp["ST"], kTm[:, tk], qTm[:, tk], True, True, [r_proj], [mp["rST"]])
                        TT("dve", Gt, m0, amx[:, i, d:d + 1], ALU.max, [R["m0"], r_gate], [R["G"]])
                        yield
                        TS("dve", nG, Gt, -1.0, ALU.mult, [R["G"]], [R["nG"]])
                        yield
                        ACT(u, at[:, i, d:d + 1], AF.Exp, [r_gate, R["nG"]], [R["u"]], bias=nG)
                        ACT(wi, m0, AF.Exp, [R["m0"], R["nG"]], [R["wi"]], bias=nG)
                        ACT(fl, Bt[:, i, d:d + 1], AF.Exp, [r_gate, R["nG"]], [R["fl"]], bias=nG)
                        yield
                        TT("dve", m0, Gt, Bt[:, i, 2 + d:3 + d], ALU.subtract, [R["G"], r_gate], [R["m0"]])
                        STT(c["b3"][:], mp["ST"], u, mask, ALU.mult, ALU.mult, [mp["rST"], R["u"], r_const], [R["b3"]])
                        TS("dve", c["b4"][:], ktm[:, i, :], u, ALU.mult, [r_proj, R["u"]], [R["b4"]])
                        ACT(c["b1"][:, 0:129], cn[:, 0:129], AF.Copy, [R["st"], R["wi"]], [R["b1"]], scale=wi)
                        yield
                        MM(mp["ND"], c["b3"][:], v1[:, i, 0:129], True, False, [R["b3"], r_proj, r_v1], [mp["rND"]])
                        MM(mp["ND"], qTm[:, tk], c["b1"][:, 0:129], False, True, [R["b1"], r_proj], [mp["rND"]])
                        MM(mp["dCN"], c["b4"][:], v1[:, i, 0:129], True, True, [R["b4"], r_proj, r_v1], [mp["rdCN"]])
                        yield
                        TS("dve", dn, mp["ND"][:, 128:129], -1.0, ALU.mult, [mp["rND"]], [R["dn"]])
                        TT("dve", dn, dn, mp["ND"][:, 128:129], ALU.max, [mp["rND"], R["dn"]], [R["dn"]])
                        STT(cn[:, 0:129], cn[:, 0:129], wi, mp["dCN"], ALU.mult, ALU.add, [mp["rdCN"], R["wi"], R["st"]], [R["st"]])
                        yield
                        TT("dve", dn, dn, fl, ALU.max, [R["dn"], R["fl"]], [R["dn"]])
                        yield
                        RECIP(rc, dn, [R["dn"]], [R["rc"]])
                        yield
                        STT(hsum[:, ig, :], mp["ND"][:, 0:128], rc, hsum[:, ig, :], ALU.mult, ALU.add,
                            [mp["rND"], R["rc"], r_hs[ig]], [r_hs[ig]])
                        yield

                def hg_init(d, c, h=h):
                    S_ = c["st"]
                    MSET("pool", c["q4"][:], 0.0, [c["R"]["q4"]])
                    if is_prompt:
                        MSET("pool", S_[:], 0.0, [c["R"]["st"]])
                    else:
                        P.dma("sp", S_[:, 0:128], st_S[d, h, :, :], writes=[c["R"]["st"]])

                def hg_final(seq, d, c, h=h):
                    P.dma("sp", o_S[seq, d, h, :, :], c["st"][:, 0:128], reads=[c["R"]["st"]], is_output=True)

                def hg_run(d, c, tiles, bs):
                    S_ = c["st"]
                    R = c["R"]
                    qTh, lfall, kcall, itm = (bs[k] for k in ("qTh", "lfall", "kcall", "itm"))
                    r_proj = bs["r_proj"]
                    hp = HGP[d]
                    su = suF if d == 0 else suB
                    m32 = m32F if d == 0 else m32B
                    corder = range(4) if d == 0 else range(3, -1, -1)
                    yield
                    for i, ig in tiles:
                        tk = slice(i * 128, (i + 1) * 128)
                        lfd = lfall[:, i, d, :]
                        MM(hp["D1"], su, lfd, True, True, [r_proj, r_const], [hp["rD1"]])
                        MM(hp["D1T"], lfd, su, True, True, [r_proj, r_const], [hp["rD1T"]])
                        MM(hp["al"], lfd, CI, True, True, [r_proj, r_const], [hp["ral"]])
                        yield
                        ACT(c["f1"][:], hp["D1"], AF.Exp, [hp["rD1"]], [R["f1"]])
                        ACT(c["f2"][:], hp["D1T"], AF.Exp, [hp["rD1T"]], [R["f2"]], scale=-1.0)
                        ACT(c["e4"][:], hp["al"], AF.Exp, [hp["ral"]], [R["e4"]])
                        yield
                        TT("dve", c["b3"][:], kcall[:, i, d, :], c["f1"][:], ALU.mult, [R["f1"], r_proj], [R["b3"]])
                        TT("dve", c["b2"][:, 0:128], qTh[:, tk], c["f2"][:], ALU.mult, [R["f2"], r_proj], [R["b2"]])
                        yield
                        TR(pbh[d], c["b3"][:], identB, [R["b3"], r_const], [pbhr[d]])
                        for j in range(4):
                            cs_ = slice(32 * j, 32 * j + 32)
                            TT("dve", c["q4"][:, j, cs_], qTh[:, 128 * i + 32 * j:128 * i + 32 * j + 32], c["f2"][:, cs_],
                               ALU.mult, [R["f2"], r_proj], [R["q4"]])
                            TS("dve", c["k4"][:, j, :], c["b3"][:], CI[:, j:j + 1], ALU.mult, [R["b3"], r_const], [R["k4"]])
                        yield
                        CP("act", c["b4"][:], pbh[d], [pbhr[d]], [R["b4"]])
                        yield
                        MM(hp["sc"], c["b4"][:], c["b2"][:, 0:128], True, True, [R["b4"], R["b2"]], [hp["rsc"]])
                        yield
                        TT("dve", c["b1"][:, 0:128], hp["sc"], m32, ALU.mult, [hp["rsc"], r_const], [R["b1"]])
                        yield
                        MM(hp["O"], c["b1"][:, 0:128], itm[:, i, :], True, False, [R["b1"], r_proj], [hp["rO"]])
                        for jn, j in enumerate(corder):
                            sbi = jn % 2
                            sbuf_, sbr_ = c["b5"][:, sbi, :], R["sb%d" % sbi]
                            ACT(sbuf_, S_[:, 0:128], AF.Copy, [R["st"], R["e4"]], [sbr_], scale=c["e4"][:, j:j + 1])
                            MM(hp["dS"][sbi], c["k4"][:, j, :], itm[:, i, :], True, True, [R["k4"], r_proj], [hp["rdS"][sbi]])
                            yield
                            MM(hp["O"], c["q4"][:, j, :], sbuf_, False, jn == 3, [R["q4"], sbr_], [hp["rO"]])
                            STT(S_[:, 0:128], S_[:, 0:128], c["e4"][:, j:j + 1], hp["dS"][sbi], ALU.mult, ALU.add,
                                [hp["rdS"][sbi], R["e4"], R["st"]], [R["st"]])
                            yield
                        TT("dve", osum[:, ig, :], osum[:, ig, :], hp["O"], ALU.add, [hp["rO"], r_os[ig]], [r_os[ig]])
                        yield

                for seq in range(nseq):
                    T0 = seq * S
                    MSET("pool", hsum[:], 0.0, r_hs)
                    MSET("pool", osum[:], 0.0, r_os)
                    for d in range(2):
                        ml_init(d, ch[d])
                        hg_init(d, ch[2 + d])
                    if nparts == 1:
                        inproj(T0, 0, PB[0])
                        fw = [(i, i) for i in range(NT)]
                        bw = [(i, i) for i in range(NT - 1, -1, -1)]
                        chain_bank_fence(False)
                        round_robin([ml_run(0, ch[0], fw, PB[0]), hg_run(0, ch[2], fw, PB[0]),
                                     ml_run(1, ch[1], bw, PB[0]), hg_run(1, ch[3], bw, PB[0])])
                        chain_bank_fence(True)
                    else:
                        if SEQ_PASSES:
                            for p_ in range(nparts):
                                inproj(T0 + p_ * SP, p_ * NTp, PB[0])
                                tl = [(i, p_ * NTp + i) for i in range(NTp)]
                                chain_bank_fence(False)
                                round_robin([ml_run(0, ch[0], tl, PB[0]), hg_run(0, ch[2], tl, PB[0])])
                                chain_bank_fence(True)
                            for p_ in range(nparts - 1, -1, -1):
                                inproj(T0 + p_ * SP, p_ * NTp, PB[0])
                                tl = [(i, p_ * NTp + i) for i in range(NTp - 1, -1, -1)]
                                chain_bank_fence(False)
                                round_robin([ml_run(1, ch[1], tl, PB[0]), hg_run(1, ch[3], tl, PB[0])])
                                chain_bank_fence(True)
                        for st_ in (range(nparts) if not SEQ_PASSES else ()):
                            pf, pb_ = st_, nparts - 1 - st_
                            inproj(T0 + pf * SP, pf * NTp, PB[0])
                            inproj(T0 + pb_ * SP, pb_ * NTp, PB[1])
                            tf = [(i, pf * NTp + i) for i in range(NTp)]
                            tb_ = [(i, pb_ * NTp + i) for i in range(NTp - 1, -1, -1)]
                            chain_bank_fence(False)
                            round_robin([ml_run(0, ch[0], tf, PB[0]), hg_run(0, ch[2], tf, PB[0]),
                                         ml_run(1, ch[1], tb_, PB[1]), hg_run(1, ch[3], tb_, PB[1])])
                            chain_bank_fence(True)
                    if is_prompt:
                        for d in range(2):
                            ml_final(seq, d, ch[d])
                            hg_final(seq, d, ch[2 + d])
                    items = []
                    for i in range(NT):
                        tk = slice(T0 + i * 128, T0 + (i + 1) * 128)
                        items.append((hsum[:, i, :], mlgT[:, hs], sgo[:, i, :], moT[:, h, tk], [r_hs[i], r_go], mores))
                        items.append((osum[:, i, :], hggT[:, hs], ohg[:, i, :], moT[:, 4 + h, tk], [r_os[i], r_go], mores))
                    head_norm_batch(hn, items)

    def odd_mixer(seg, hT, hres, moT, mores, nseq, S, is_prompt, hTq=None, hqres=None, SQ=None):
        NT = S // 128
        rope = not is_prompt
        NKT = NT if is_prompt else NT + PAST // 128
        koff = 0 if is_prompt else PAST // 128
        QN = S if hTq is None else SQ
        with phase() as ph:
            qT = sb(ph, "a_qT", [128, nseq * QN], BF16)
            qT2 = sb(ph, "a_qT2", [128, nseq * QN], BF16)
            kT = sb(ph, "a_kT", [128, nseq * NKT * 128], BF16)
            v1h = sb(ph, "a_v1", [128, nseq * NKT, 130], BF16)
            kvst = sb(ph, "a_kvst", [128, 2, 256], F32)
            r_kvst = [Res(), Res()]
            pe_ = sb(ph, "a_p", [128, 2, 2, 256], BF16)
            fin = sb(ph, "a_fin", [128, 8], F32)
            tt_ = sb(ph, "a_t", [128, 128], F32)
            att = sb(ph, "a_att", [128, nseq * (QN // 128), 128], F32)
            r_att = [Res() for _ in range(nseq * (QN // 128))]
            hn = mk_hnb(ph)
            r_q, r_k, r_v, r_fin = Res(), Res(), Res(), Res()
            r_p = [Res(), Res()]
            if rope:
                cq = sb(ph, "a_cq", [128, QN], F32)
                sq_ = sb(ph, "a_sq", [128, QN], F32)
                ck = sb(ph, "a_ck", [128, 512], F32)
                sk = sb(ph, "a_sk", [128, 512], F32)
                xb = sb(ph, "a_xb", [128, 512], BF16)
                t1 = sb(ph, "a_t1", [128, 512], F32)
                t2 = sb(ph, "a_t2", [128, 512], F32)
                r_cq, r_ck, r_rp = Res(), Res(), Res()
                P.dma("sp", cq[:], c_cosq[:, :], writes=[r_cq])
                P.dma("sp", sq_[:], c_sinq[:, :], writes=[r_cq])

                def rope_block(pb, pr, n, cos_ap, sin_ap, cres, dsts):
                    CP("act", xb[:, :n], pb[:, :n], [pr], [r_rp])
                    pb2, pr2 = next_bank()
                    MM(pb2[:, :n], cRot[:], xb[:, :n], True, True, [r_rp, r_const], [pr2])
                    TT("dve", t1[:, :n], pb[:, :n], cos_ap, ALU.mult, [pr, cres], [r_rp])
                    TT("dve", t2[:, :n], pb2[:, :n], sin_ap, ALU.mult, [pr2, cres], [r_rp])
                    for dst, ps_, dres in dsts:
                        TT("pool", dst, t1[ps_, :n], t2[ps_, :n], ALU.add, [r_rp], [dres])
            for i in range(nseq * NKT):
                MSET("pool", v1h[:, i, 128:130], 1.0, [r_v])
            MSET("pool", qT[64:128, :], 0.0, [r_q])
            MSET("pool", qT2[0:64, :], 0.0, [r_q])
            stc = [0]
            def load_head_w(h):
                wt, wr = wring[wctr[0] % NWB], wres[wctr[0] % NWB]
                wctr[0] += 1
                wv = wview(wt, KC, 384)
                for j in range(3):
                    wpart(wv[:, :, j * 128:(j + 1) * 128], od_w_in[:, j * D + h * 128:j * D + (h + 1) * 128], KC, 128, wr)
                return wv, wr

            next_w = load_head_w(0)
            for h in range(8):
                wv, wr = next_w
                if h + 1 < 8:
                    next_w = load_head_w(h + 1)
                if not is_prompt:
                    wpart(kT[:, 0:PAST].rearrange("p (k n) -> p k n", k=1), ckT[h, :, :], 1, PAST, r_k)
                    for kt in range(PAST // 128):
                        wpart(v1h[:, kt:kt + 1, 0:128], cv[kt * 128:(kt + 1) * 128, h * 128:(h + 1) * 128], 1, 128, r_v)
                for seq in range(nseq):
                    T0 = seq * S
                    qoff, kbase, vt0 = seq * QN, seq * NKT * 128, seq * NKT
                    qsrc, qres_, Q0 = (hT, hres, T0) if hTq is None else (hTq, hqres, 0)
                    for tb in range(0, QN, 512):
                        n = min(512, QN - tb)
                        pb, pr = next_bank()
                        for kc in range(KC):
                            MM(pb[:, :n], wv[:, kc, 0:128], qsrc[:, kc, Q0 + tb:Q0 + tb + n],
                               kc == 0, kc == KC - 1, [wr, qres_], [pr])
                        if rope:
                            rope_block(pb, pr, n, cq[:, tb:tb + n], sq_[:, tb:tb + n], r_cq,
                                       [(qT[0:64, qoff + tb:qoff + tb + n], slice(0, 64), r_q),
                                        (qT2[64:128, qoff + tb:qoff + tb + n], slice(64, 128), r_q)])
                        else:
                            CP("act", qT[0:64, qoff + tb:qoff + tb + n], pb[0:64, :n], [pr], [r_q])
                            CP("act", qT2[64:128, qoff + tb:qoff + tb + n], pb[64:128, :n], [pr], [r_q])
                    for tb in range(0, S, 512):
                        n = min(512, S - tb)
                        pb, pr = next_bank()
                        for kc in range(KC):
                            MM(pb[:, :n], wv[:, kc, 128:256], hT[:, kc, T0 + tb:T0 + tb + n],
                               kc == 0, kc == KC - 1, [wr, hres], [pr])
                        kd = kT[:, kbase + koff * 128 + tb:kbase + koff * 128 + tb + n]
                        if rope:
                            P.dma("sp", ck[:, :n], c_cosk[:, tb:tb + n], writes=[r_ck])
                            P.dma("sp", sk[:, :n], c_sink[:, tb:tb + n], writes=[r_ck])
                            rope_block(pb, pr, n, ck[:, :n], sk[:, :n], r_ck, [(kd, slice(0, 128), r_k)])
                        else:
                            CP("act", kd, pb[:, :n], [pr], [r_k])
                    for i in range(NT):
                        tk = slice(T0 + i * 128, T0 + (i + 1) * 128)
                        pb, pr = next_bank()
                        for kc in range(KC):
                            MM(pb[:, 0:256], hT[:, kc, tk], wv[:, kc, 128:384], kc == 0, kc == KC - 1, [wr, hres], [pr])
                        si = stc[0] % 2
                        stc[0] += 1
                        CP("act", kvst[:, si, :], pb[:, 0:256], [pr], [r_kvst[si]])
                        CP("dve", v1h[:, vt0 + koff + i, 0:128], kvst[:, si, 128:256], [r_kvst[si]], [r_v])
                        if is_prompt:
                            P.dma("sp", o_k[T0 + i * 128:T0 + (i + 1) * 128, h * 128:(h + 1) * 128], kvst[:, si, 0:128],
                                  reads=[r_kvst[si]], is_output=True)
                            P.dma("sp", o_v[T0 + i * 128:T0 + (i + 1) * 128, h * 128:(h + 1) * 128], kvst[:, si, 128:256],
                                  reads=[r_kvst[si]], is_output=True)
                hitems = []
                for seq in range(nseq):
                    T0 = seq * S
                    qoff, kbase, vt0 = seq * QN, seq * NKT * 128, seq * NKT
                    Q0 = T0 if hTq is None else 0
                    for qb in range(0, QN, 256):
                        acc = [(pbank[0], pres[0]), (pbank[1], pres[1])]
                        def emit_scores(kt, qb=qb, qoff=qoff, kbase=kbase):
                            ks = slice(kbase + kt * 128, kbase + (kt + 1) * 128)
                            b0 = 2 if kt % 2 == 0 else 4
                            MM(pbank[b0][:, 0:256], kT[:, ks], qT[:, qoff + qb:qoff + qb + 256], True, True, [r_k, r_q],
                               [pres[b0]])
                            MM(pbank[b0 + 1][:, 0:256], kT[:, ks], qT2[:, qoff + qb:qoff + qb + 256], True, True, [r_k, r_q],
                               [pres[b0 + 1]])
                        emit_scores(0)
                        for kt in range(NKT):
                            b0 = 2 if kt % 2 == 0 else 4
                            sa, sar = pbank[b0], pres[b0]
                            sb2, sbr = pbank[b0 + 1], pres[b0 + 1]
                            pi = kt % 2
                            if kt + 1 < NKT:
                                emit_scores(kt + 1)
                            ACT(pe_[:, pi, 0, :], sa[:, 0:256], AF.Exp, [sar], [r_p[pi]], scale=0.125)
                            ACT(pe_[:, pi, 1, :], sb2[:, 0:256], AF.Exp, [sbr], [r_p[pi]], scale=0.125)
                            for qs in range(2):
                                ab, ar = acc[qs]
                                MM(ab[:, 0:129], pe_[:, pi, 0, qs * 128:(qs + 1) * 128], v1h[:, vt0 + kt, 0:129],
                                   kt == 0, False, [r_p[pi], r_v], [ar])
                                MM(ab[:, 129:258], pe_[:, pi, 1, qs * 128:(qs + 1) * 128], v1h[:, vt0 + kt, 0:129],
                                   False, kt == NKT - 1, [r_p[pi], r_v], [ar])
                        for qs in range(2):
                            ab, ar = acc[qs]
                            RECIP(fin[:, 0:1], ab[:, 128:129], [ar], [r_fin])
                            RECIP(fin[:, 1:2], ab[:, 257:258], [ar], [r_fin])
                            TT("dve", fin[:, 2:3], fin[:, 1:2], lamv[:, 1:2], ALU.mult, [r_fin, r_par], [r_fin])
                            TS("dve", tt_[:], ab[:, 129:257], fin[:, 2:3], ALU.mult, [ar, r_fin], [r_fin])
                            ai = seq * (QN // 128) + qb // 128 + qs
                            STT(att[:, ai, :], ab[:, 0:128], fin[:, 0:1], tt_[:], ALU.mult, ALU.add, [ar, r_fin, r_att[ai]],
                                [r_att[ai]])
                            tk = slice(Q0 + qb + qs * 128, Q0 + qb + (qs + 1) * 128)
                            hitems.append((att[:, ai, :], dagS[:], None, moT[:, h, tk], [r_att[ai]], mores))
                head_norm_batch(hn, hitems)

    def odd_mixer_prompt(seg, hT, hres, moT, mores):
        odd_mixer(seg, hT, hres, moT, mores, PSEQ, PS, True)

    def segment_sample():
        w = 1
        T = SS
        with phase() as seg:
            buf2 = sb(seg, "buf2", [128, KC, T], BF16)
            selT = sb(seg, "selT", [128, 4], F32)
            r_own, r_hq, r_moq, r_b2, r_sel = Res(), Res(), Res(), Res(), Res()
            P.dma("sp", selT[:], sel[:, :], writes=[r_sel])
            with phase() as H:
                hT = sb(H, "s_hT", [128, KC, T], BF16)
                hres = Res()
                with phase() as Ln:
                    xt = sb(Ln, "s_xt", [128, KC, 512], F32)
                    r_xt = Res()
                    scr = mk_norm_scr(Ln)
                    for t0 in range(0, T, 512):
                        for kc in range(KC):
                            P.dma("sp", xt[:, kc, :], xsT[kc * 128:(kc + 1) * 128, t0:t0 + 512], writes=[r_xt])
                        norm_mod(scr, xt, r_xt, 0, 512, gm[:, 0, 0, w, :], mvec(0, 0, w), hT, t0, hres)
                even_mixer(H, hT, hres, buf2, r_b2, 1, SS, False, SP=512)
            x1own = sb(seg, "x1own", [128, KC, NQ], F32)
            hTq = sb(seg, "hTq", [128, KC, NQ], BF16)
            with phase() as X:
                x1T = sb(X, "x1T", [128, KC, T], F32)
                r_x1 = Res()
                load_xT(x1T, r_x1, xsT, T)
                linear_fm(ev_w_out, KC, D, buf2, r_b2, T, residual_epilogue(x1T, r_x1, mvec(0, 2, w)))
                with phase() as Ln:
                    scr = mk_norm_scr(Ln)
                    for t0 in range(0, T, 512):
                        norm_mod(scr, x1T, r_x1, t0, 512, gm[:, 0, 1, w, :], mvec(0, 3, w), buf2, t0, r_b2)
                with phase() as F_:
                    ffn(F_, 0, buf2, r_b2, T, x1T, r_x1, mvec(0, 5, w))
                with phase() as Ln:
                    scr = mk_norm_scr(Ln)
                    for t0 in range(0, T, 512):
                        norm_mod(scr, x1T, r_x1, t0, 512, gm[:, 1, 0, w, :], mvec(1, 0, w), buf2, t0, r_b2)
                for kc in range(KC):
                    for j in range(4):
                        js = slice(j * NQ, (j + 1) * NQ)
                        if j == 0:
                            TS("dve", x1own[:, kc, :], x1T[:, kc, js], selT[:, 0:1], ALU.mult, [r_x1, r_sel], [r_own])
                            TS("pool", hTq[:, kc, :], buf2[:, kc, js], selT[:, 0:1], ALU.mult, [r_b2, r_sel], [r_hq])
                        else:
                            STT(x1own[:, kc, :], x1T[:, kc, js], selT[:, j:j + 1], x1own[:, kc, :], ALU.mult, ALU.add,
                                [r_x1, r_sel, r_own], [r_own])
                            STT(hTq[:, kc, :], buf2[:, kc, js], selT[:, j:j + 1], hTq[:, kc, :], ALU.mult, ALU.add,
                                [r_b2, r_sel, r_hq], [r_hq])
            moTq = sb(seg, "moTq", [128, KC, NQ], BF16)
            odd_mixer(seg, buf2, r_b2, moTq, r_moq, 1, SS, False, hTq=hTq, hqres=r_hq, SQ=NQ)
            linear_fm(od_w_out, KC, D, moTq, r_moq, NQ, residual_epilogue(x1own, r_own, mvec(1, 2, w)))
            with phase() as Ln:
                scr = mk_norm_scr(Ln)
                norm_mod(scr, x1own, r_own, 0, NQ, gm[:, 1, 1, w, :], mvec(1, 3, w), hTq, 0, r_hq)
            with phase() as F_:
                ffn(F_, 1, hTq, r_hq, NQ, x1own, r_own, mvec(1, 5, w))
            with phase() as Ln:
                scr = mk_norm_scr(Ln)
                final_norm_out(Ln, scr, x1own, r_own, NQ, ysT)


    class phase:
        def __enter__(self):
            self.st = ExitStack()
            return self.st
        def __exit__(self, *a):
            P.barrier()
            self.st.close()
            return False

    def load_xT(dst, dres, src, T):
        for kc in range(KC):
            P.dma("sp", dst[:, kc, :T], src[kc * 128:(kc + 1) * 128, 0:T], writes=[dres])

    def stub_mixer(hT, hres, moT, mores, T):
        for kc in range(KC):
            P.op("pool", lambda e, kc=kc: e.tensor_copy(out=moT[:, kc, :T], in_=hT[:, kc, :T]), [hres], [mores])

    def final_norm_out(seg, scr, xT, xres, T, out_ap):
        fin = sb(seg, "fin", [128, KC], F32)
        r_fin = Res()
        P.dma("sp", fin[:], finT[:, :], writes=[r_fin])
        yst = sb(seg, "yst", [128, KC, 512], F32)
        r_y = Res()
        for t0 in range(0, T, 512):
            norm_mod(scr, xT, xres, t0, 512, fin, None, yst, 0, r_y, extra_reads=[r_fin])
            for kc in range(KC):
                P.dma("sp", out_ap[kc * 128:(kc + 1) * 128, t0:t0 + 512], yst[:, kc, :], reads=[r_y], is_output=True)

    def segment_prompt():
        T = PSEQ * PS
        w = 0
        with phase() as seg:
            xT = sb(seg, "xT", [128, KC, T], F32)
            hT = sb(seg, "hT", [128, KC, T], BF16)
            moT = sb(seg, "moT", [128, KC, T], BF16)
            xres, hres, mores = Res(), Res(), Res()
            load_xT(xT, xres, xpT, T)
            scr = mk_norm_scr(seg)
            for l in range(2):
                for t0 in range(0, T, 512):
                    norm_mod(scr, xT, xres, t0, 512, gm[:, l, 0, w, :], mvec(l, 0, w), hT, t0, hres)
                if l == 0:
                    if STAGE >= 2:
                        even_mixer(seg, hT, hres, moT, mores, PSEQ, PS, True)
                    else:
                        stub_mixer(hT, hres, moT, mores, T)
                    if DEBUG:
                        for kc in range(KC):
                            P.dma("sp", dbgT[kc * 128:(kc + 1) * 128, :], moT[:, kc, :], reads=[mores], is_output=True)
                    linear_fm(ev_w_out, KC, D, moT, mores, T, residual_epilogue(xT, xres, mvec(l, 2, w)))
                else:
                    if STAGE >= 3:
                        odd_mixer_prompt(seg, hT, hres, moT, mores)
                    else:
                        stub_mixer(hT, hres, moT, mores, T)
                    linear_fm(od_w_out, KC, D, moT, mores, T, residual_epilogue(xT, xres, mvec(l, 2, w)))
                for t0 in range(0, T, 512):
                    norm_mod(scr, xT, xres, t0, 512, gm[:, l, 1, w, :], mvec(l, 3, w), hT, t0, hres)
                with phase() as ph:
                    ffn(ph, l, hT, hres, T, xT, xres, mvec(l, 5, w), TILE=1024)
            final_norm_out(seg, scr, xT, xres, T, ypT)

    phase_adaln()
    phase_params()
    segment_prompt()
    if STAGE >= 4:
        segment_sample()
    _LAST['P'] = P
    _LAST['nwc'] = len(wcache)
    P.finish()
    return nc


def _consts():
    i = np.arange(128)
    s, t = i[:, None], i[None, :]
    same32 = (s // 32) == (t // 32)
    cf = np.zeros((128, 7, 128), np.float32)
    cf[:, 0] = np.eye(128)
    cf[:, 1] = (s <= t)
    cf[:, 2] = (s >= t)
    cf[:, 3] = 1.0
    cf[:, 4] = same32 & (s > t)
    cf[:, 5] = same32 & (s < t)
    cf[:, 6, 0:4] = (s // 32) == np.arange(4)[None, :]
    cb = np.zeros((128, 6, 128), np.float32)
    cb[:, 0] = np.eye(128)
    cb[:, 1] = 1.0
    cb[:, 2] = (s <= t)
    cb[:, 3] = (s >= t)
    cb[:, 4] = same32 & (s <= t)
    cb[:, 5] = same32 & (s >= t)
    R = np.zeros((128, 128), np.float32)
    for d in range(128):
        if d % 32 < 16:
            R[d, d + 16] = -1.0
        else:
            R[d, d - 16] = 1.0
    rot = np.ascontiguousarray(R.T)
    return cf, cb, rot


def _rope_tables(positions):
    d = np.arange(128)
    inv = (10000.0 ** (-(np.arange(16, dtype=np.float32)) / 16.0)).astype(np.float32)
    row = (positions // 64).astype(np.float32)
    col = (positions % 64).astype(np.float32)
    pos = np.where(((d % 64) < 32)[:, None], row[None, :], col[None, :]).astype(np.float32)
    ang = (pos * inv[d % 16][:, None]).astype(np.float32)
    return np.cos(ang).astype(np.float32), np.sin(ang).astype(np.float32)


_NC_CACHE = {}


def _get_nc():
    if "nc" not in _NC_CACHE:
        _NC_CACHE["nc"] = build_program()
    return _NC_CACHE["nc"]


def make_in_maps(x_prompt, x_sample, c, c_ctx, cache_attn_k, cache_attn_v, state_mlstm_C, state_mlstm_n,
                 state_mlstm_m, state_hgrn_S, ada_w, ada_b, norm_mix_g, norm_ffn_g, ev_w_in, ev_gate_b,
                 ev_lb_logits, ml_norm_g, hg_norm_g, ev_w_out, od_w_in, od_lambda, da_norm_g, od_w_out,
                 ffn_w1, ffn_w3, ffn_w2, final_norm_g):
    f = lambda a: np.ascontiguousarray(np.asarray(a, dtype=np.float32))
    cf, cb, rot = _consts()
    cosk, sink = _rope_tables(np.arange(SS))
    shared = {
        "ada_w": f(ada_w),
        "ada_bT": f(np.asarray(ada_b).reshape(2, 48, 128).transpose(2, 0, 1)),
        "nmixT": f(np.asarray(norm_mix_g).reshape(2, KC, 128).transpose(2, 0, 1)),
        "nffnT": f(np.asarray(norm_ffn_g).reshape(2, KC, 128).transpose(2, 0, 1)),
        "finT": f(np.asarray(final_norm_g).reshape(KC, 128).T),
        "ev_w_in": f(np.asarray(ev_w_in)[0]),
        "gate_b": f(np.broadcast_to(np.asarray(ev_gate_b)[0][None, :], (128, 16))),
        "lb_log": f(np.broadcast_to(np.asarray(ev_lb_logits)[None, :, :], (128, 2, 512))),
        "mlg_g": f(np.broadcast_to(np.asarray(ml_norm_g)[0][None, :], (128, 512))),
        "hgg_g": f(np.broadcast_to(np.asarray(hg_norm_g)[0][None, :], (128, 512))),
        "ev_w_out": f(np.asarray(ev_w_out)[0]),
        "od_w_in": f(np.asarray(od_w_in)[0]),
        "od_lam": f(np.broadcast_to(np.asarray(od_lambda)[0][None, :, :], (128, 4, 64))),
        "da_g": f(np.broadcast_to(np.asarray(da_norm_g)[0][None, :], (128, 128))),
        "od_w_out": f(np.asarray(od_w_out)[0]),
        "ffn_w1": f(ffn_w1), "ffn_w3": f(ffn_w3), "ffn_w2": f(ffn_w2),
        "c_cosk": cosk, "c_sink": sink, "c_f32": cf, "c_bf": cb, "c_rot": rot,
    }
    xp = np.asarray(x_prompt, dtype=np.float32)
    xs = np.asarray(x_sample, dtype=np.float32)
    maps = []
    for core in range(NCORE):
        b, j = core // 4, core % 4
        m = dict(shared)
        m["xpT"] = f(xp[PSEQ * core:PSEQ * (core + 1)].reshape(PSEQ * PS, D).T)
        m["xsT"] = f(xs[b].T)
        cond = np.stack([np.asarray(c_ctx, np.float32), np.asarray(c, np.float32)[b]], axis=-1)
        m["condT"] = f(cond.reshape(KC, 128, 2).transpose(1, 0, 2))
        m["ckT"] = f(np.asarray(cache_attn_k)[b, 0].transpose(1, 2, 0))
        m["cv"] = f(np.asarray(cache_attn_v)[b, 0].reshape(PAST, D))
        m["st_C"] = f(np.asarray(state_mlstm_C)[b, 0])
        m["st_n"] = f(np.asarray(state_mlstm_n)[b, 0].reshape(8, 128))
        m["st_m"] = f(np.broadcast_to(np.asarray(state_mlstm_m)[b, 0].reshape(8, 1), (8, 128)))
        m["st_S"] = f(np.asarray(state_hgrn_S)[b, 0])
        selv = np.zeros((128, 4), np.float32)
        selv[:, j] = 1.0
        m["sel"] = selv
        cq, sq = _rope_tables(np.arange(j * NQ, (j + 1) * NQ))
        m["c_cosq"], m["c_sinq"] = cq, sq
        maps.append(m)
    return maps


def assemble(results):
    B = NCORE * PSEQ
    y_prompt = np.zeros((B, PS, D), np.float32)
    y_sample = np.zeros((2, SS, D), np.float32)
    nk = np.zeros((B, 1, PS, 8, 128), np.float32)
    nv = np.zeros((B, 1, PS, 8, 128), np.float32)
    nC = np.zeros((B, 1, 2, 4, 128, 128), np.float32)
    nn = np.zeros((B, 1, 2, 4, 128), np.float32)
    nm = np.zeros((B, 1, 2, 4), np.float32)
    nS = np.zeros((B, 1, 2, 4, 128, 128), np.float32)
    for core in range(NCORE):
        r = results[core]
        b, j = core // 4, core % 4
        sl = slice(PSEQ * core, PSEQ * (core + 1))
        y_prompt[sl] = np.asarray(r["ypT"]).T.reshape(PSEQ, PS, D)
        y_sample[b, j * NQ:(j + 1) * NQ] = np.asarray(r["ysT"]).T
        nk[sl, 0] = np.asarray(r["o_k"]).reshape(PSEQ, PS, 8, 128)
        nv[sl, 0] = np.asarray(r["o_v"]).reshape(PSEQ, PS, 8, 128)
        nC[sl, 0] = np.asarray(r["o_C"])
        nn[sl, 0] = np.asarray(r["o_n"])
        nm[sl, 0] = np.asarray(r["o_m"])
        nS[sl, 0] = np.asarray(r["o_S"])
    return (y_prompt, y_sample, nk, nv, nC, nn, nm, nS)


def kernel(**inputs):
    nc = _get_nc()
    maps = make_in_maps(**inputs)
    res = run_bass_kernel_spmd(nc, maps, core_ids=list(range(NCORE)))
    _LAST["res"] = res.results
    return assemble(res.results)
```

```python
import math
from contextlib import ExitStack
import numpy as np
import ml_dtypes
import concourse.bass as bass
import concourse.mybir as mybir
from concourse.bass_utils import run_bass_kernel_spmd

F32 = mybir.dt.float32
BF16 = mybir.dt.bfloat16
AF = mybir.ActivationFunctionType
ALU = mybir.AluOpType
AX = mybir.AxisListType

D = 1024
KC = 8
DFF = 2816
FC = 22
EPS = 1e-6
NCORE = 8
PSEQ = 4
PS = 256
SS = 2048
PAST = 256
NQ = 512
EV_IN = 4624
OFF_MLQ, OFF_MLK, OFF_MLV, OFF_MLO, OFF_MLG = 0, 512, 1024, 1536, 2048
OFF_HGQ, OFF_HGFF, OFF_HGFB, OFF_HGI, OFF_HGO = 2064, 2576, 3088, 3600, 4112
LAM_INIT1 = 0.8 - 0.6 * math.exp(-0.3 * 1)

ENGS = ("pe", "act", "dve", "pool", "sp")


class Res:
    __slots__ = ("lw", "rd", "excl")

    def __init__(self, excl=False):
        self.lw = None
        self.rd = {}
        self.excl = excl


class Prog:
    def __init__(self, nc, es, ndma=8):
        self.nc = nc
        self.q = {e: [] for e in ENGS}
        self.cnt = {}
        self.seen = {e: {} for e in ENGS}
        self.sem = {}
        self.ndma = ndma
        self.rr = {e: 0 for e in ENGS}
        for e in ENGS:
            self.sem[("e", e)] = es.enter_context(nc.semaphore("s_" + e))
        for e in ("sp", "pool", "act"):
            for i in range(ndma):
                self.sem[("d", e, i)] = es.enter_context(nc.semaphore("d_%s%d" % (e, i)))
        self.out_tokens = []

    def _deps(self, eng, reads, writes):
        need = {}

        def add(tok):
            if tok is None:
                return
            k, v = tok
            if eng == "pe" and k == ("e", "pe"):
                return
            if need.get(k, 0) < v:
                need[k] = v

        for r in reads:
            add(r.lw)
            if r.excl:
                for k, v in r.rd.items():
                    if k != ("e", eng):
                        add((k, v))
        for w in writes:
            add(w.lw)
            for k, v in w.rd.items():
                add((k, v))
        out = []
        for k, v in need.items():
            if self.seen[eng].get(k, 0) >= v:
                continue
            self.seen[eng][k] = v
            out.append((k, v))
        return out

    def _commit(self, tok, reads, writes):
        for w in writes:
            w.lw = tok
            w.rd = {}
        for r in reads:
            if r in writes:
                continue
            k, v = tok
            if r.rd.get(k, 0) < v:
                r.rd[k] = v

    def op(self, eng, fn, reads=(), writes=()):
        waits = self._deps(eng, reads, writes)
        k = ("e", eng)
        self.cnt[k] = self.cnt.get(k, 0) + 1
        tok = (k, self.cnt[k])
        self.q[eng].append((waits, fn, k, 1))
        self._commit(tok, reads, writes)
        return tok

    def dma(self, eng, out, in_, reads=(), writes=(), is_output=False):
        i = self.rr[eng]
        self.rr[eng] = (i + 1) % self.ndma
        k = ("d", eng, i)
        prev = self.cnt.get(k, 0)
        waits = self._deps(eng, reads, writes)
        if prev > 0 and self.seen[eng].get(k, 0) < prev:
            self.seen[eng][k] = prev
            waits.append((k, prev))
        self.cnt[k] = prev + 16
        tok = (k, prev + 16)
        self.q[eng].append((waits, (lambda e: e.dma_start(out=out, in_=in_)), k, 16))
        self._commit(tok, reads, writes)
        if is_output:
            self.out_tokens.append(tok)
        return tok

    def fence(self, eng, res_wait, res_reset):
        waits = self._deps(eng, [], res_wait)
        if waits:
            self.q[eng].append((waits, None, None, 0))
        for r in list(res_wait) + list(res_reset):
            r.lw = None
            r.rd = {}

    def barrier(self):
        cur = dict(self.cnt)
        for e in ENGS:
            waits = []
            for k, v in cur.items():
                if k == ("e", e):
                    continue
                if self.seen[e].get(k, 0) >= v:
                    continue
                self.seen[e][k] = v
                waits.append((k, v))
            if waits:
                self.q[e].append((waits, None, None, 0))

    def finish(self):
        cur = dict(self.cnt)
        waits = [(k, v) for k, v in cur.items() if k != ("e", "sp")]
        self.q["sp"].append((waits, None, None, 0))
        nc = self.nc
        with nc.Block() as block:
            for eng, deco in (("pe", block.tensor), ("act", block.scalar), ("dve", block.vector),
                              ("pool", block.gpsimd), ("sp", block.sync)):
                def body(e, eng=eng):
                    for waits, fn, k, inc in self.q[eng]:
                        for wk, wv in waits:
                            e.wait_ge(self.sem[wk], wv)
                        if fn is not None:
                            fn(e).then_inc(self.sem[k], inc)
                deco(body)


STAGE = 4
_LAST = {}
WCACHE = True
SEQ_PASSES = False
DEBUG = False
SKIP_NM = False
ODD_LEVEL = 4
ODD_OUT = True


def build_program(dbg=()):
    nc = bass.Bass("TRN2", target_bir_lowering=False)
    es = ExitStack()
    P = Prog(nc, es)
    dbg_out = {}

    def din(name, shape, dt=F32):
        return nc.dram_tensor(name, list(shape), dt, kind="ExternalInput").ap()

    def dout(name, shape, dt=F32):
        return nc.dram_tensor(name, list(shape), dt, kind="ExternalOutput").ap()

    _uid = [0]

    def sb(stack, name, shape, dt=F32):
        _uid[0] += 1
        return stack.enter_context(nc.sbuf_tensor("%s_%d" % (name, _uid[0]), list(shape), dt))

    xpT = din("xpT", [D, PSEQ * PS])
    xsT = din("xsT", [D, SS])
    condT = din("condT", [128, KC, 2])
    ada_w = din("ada_w", [2, D, 6 * D])
    ada_bT = din("ada_bT", [128, 2, 48])
    nmixT = din("nmixT", [128, 2, KC])
    nffnT = din("nffnT", [128, 2, KC])
    finT = din("finT", [128, KC])
    ev_w_in = din("ev_w_in", [D, EV_IN])
    gate_b = din("gate_b", [128, 16])
    lb_log = din("lb_log", [128, 2, 512])
    mlg_g = din("mlg_g", [128, 512])
    hgg_g = din("hgg_g", [128, 512])
    ev_w_out = din("ev_w_out", [D, D])
    od_w_in = din("od_w_in", [D, 3 * D])
    od_lam = din("od_lam", [128, 4, 64])
    da_g = din("da_g", [128, 128])
    od_w_out = din("od_w_out", [D, D])
    ffn_w1 = din("ffn_w1", [2, D, DFF])
    ffn_w3 = din("ffn_w3", [2, D, DFF])
    ffn_w2 = din("ffn_w2", [2, DFF, D])
    ckT = din("ckT", [8, 128, PAST])
    cv = din("cv", [PAST, D])
    st_C = din("st_C", [2, 4, 128, 128])
    st_n = din("st_n", [8, 128])
    st_m = din("st_m", [8, 128])
    st_S = din("st_S", [2, 4, 128, 128])
    sel = din("sel", [128, 4])
    c_cosk = din("c_cosk", [128, SS])
    c_sink = din("c_sink", [128, SS])
    c_cosq = din("c_cosq", [128, NQ])
    c_sinq = din("c_sinq", [128, NQ])
    c_f32 = din("c_f32", [128, 7, 128])
    c_bf = din("c_bf", [128, 6, 128])
    c_rot = din("c_rot", [128, 128])

    dbgT = dout("dbgT", [D, PSEQ * PS], BF16) if DEBUG else None
    ypT = dout("ypT", [D, PSEQ * PS])
    ysT = dout("ysT", [D, NQ])
    o_k = dout("o_k", [PSEQ * PS, D])
    o_v = dout("o_v", [PSEQ * PS, D])
    o_C = dout("o_C", [PSEQ, 2, 4, 128, 128])
    o_n = dout("o_n", [PSEQ, 2, 4, 128])
    o_m = dout("o_m", [PSEQ, 2, 4])
    o_S = dout("o_S", [PSEQ, 2, 4, 128, 128])

    G = es
    cF = sb(G, "cF", [128, 7, 128], F32)
    cB = sb(G, "cB", [128, 6, 128], BF16)
    cRot = sb(G, "cRot", [128, 128], BF16)
    r_const = Res()
    P.dma("sp", cF[:], c_f32[:, :, :], writes=[r_const])
    P.dma("pool", cB[:], c_bf[:, :, :], writes=[r_const])
    P.dma("pool", cRot[:], c_rot[:, :], writes=[r_const])
    epsT = sb(G, "epsT", [128, 4], F32)
    P.op("dve", lambda e: e.memset(epsT[:, 0:1], EPS), [], [r_const])
    P.op("dve", lambda e: e.memset(epsT[:, 1:2], 1.0), [], [r_const])
    P.op("dve", lambda e: e.memset(epsT[:, 2:3], 0.0), [], [r_const])
    identF, triF, triB, onesF, suF, suB = (cF[:, i, :] for i in range(6))
    CI = cF[:, 6, 0:4]
    identB, onesB, maskF, maskB, m32F, m32B = (cB[:, i, :] for i in range(6))

    pbank = [es.enter_context(nc.psum_tensor("pb%d" % i, [128, 512], F32)) for i in range(7)]
    pres = [Res(True) for _ in range(7)]
    pbf = es.enter_context(nc.psum_tensor("pbf", [128, 1024], BF16))
    pbf_res = [Res(True)] * 2

    NWB = 3
    WEL = 4096
    wring = [sb(G, "wr%d" % i, [128, WEL], BF16) for i in range(NWB)]
    wres = [Res() for _ in range(NWB)]
    wctr = [0]

    NST = 2
    STEL = 2816
    wstage = [sb(G, "wst%d" % i, [128, STEL], F32) for i in range(NST)]
    wstres = [Res() for _ in range(NST)]
    stctr = [0]
    qctr = [0]
    cctr = [0]

    wcache = {}

    def wpart(dst_view, src_ap, k, n, dres):
        assert k * n <= STEL
        key = repr(src_ap)
        if WCACHE and key in wcache:
            cap, cres = wcache[key]
            _LAST["hits"] = _LAST.get("hits", 0) + 1
            q = ("sp", "act")[qctr[0] % 2]
            qctr[0] += 1
            P.dma(q, dst_view, cap.rearrange("p (k n) -> p k n", n=n), reads=[cres], writes=[dres])
            return
        i = stctr[0] % NST
        stctr[0] += 1
        st = wstage[i][:, 0:k * n].rearrange("p (k n) -> p k n", n=n)
        q = ("sp", "act")[qctr[0] % 2]
        qctr[0] += 1
        P.dma(q, st, src_ap.rearrange("(k p) n -> p k n", p=128), writes=[wstres[i]])
        ce = ("act", "dve")[cctr[0] % 2]
        cctr[0] += 1
        if ce == "act":
            P.op("act", lambda e: e.activation(out=dst_view, in_=st, func=AF.Copy), [wstres[i]], [dres])
        else:
            P.op(ce, lambda e: e.tensor_copy(out=dst_view, in_=st), [wstres[i]], [dres])
        if WCACHE:
            cap = nc.dram_tensor("wc%d" % len(wcache), [128, k * n], BF16, kind="Internal").ap()
            cres = Res()
            wcache[key] = (cap, cres)
            q2 = ("act", "sp")[qctr[0] % 2]
            P.dma(q2, cap.rearrange("p (k n) -> p k n", n=n), dst_view, reads=[dres], writes=[cres])

    def wload(parts):
        i = wctr[0] % NWB
        wctr[0] += 1
        t, r = wring[i], wres[i]
        for src, kc, off, n in parts:
            dst = t[:, off:off + kc * n].rearrange("p (k n) -> p k n", n=n)
            nsub = 1
            while kc * (n // nsub) > STEL:
                nsub *= 2
            ns = n // nsub
            for s_ in range(nsub):
                wpart(dst[:, :, s_ * ns:(s_ + 1) * ns], src[:, s_ * ns:(s_ + 1) * ns], kc, ns, r)
        return t, r

    def wview(t, kc, n, off=0):
        return t[:, off:off + kc * n].rearrange("p (k n) -> p k n", n=n)

    modv = sb(G, "modv", [128, 2, 48, 2], F32)
    gm = sb(G, "gm", [128, 2, 2, 2, KC], F32)
    r_mod = Res()

    def phase_adaln():
        with ExitStack() as ph:
            cs = sb(ph, "cs", [128, KC, 2], F32)
            sg = sb(ph, "sg", [128, KC, 2], F32)
            csb = sb(ph, "csb", [128, KC, 2], BF16)
            abT = sb(ph, "abT", [128, 2, 48], F32)
            nm = sb(ph, "nm", [128, 2, KC], F32)
            nf = sb(ph, "nf", [128, 2, KC], F32)
            r_c = Res()
            P.dma("sp", cs[:], condT[:, :, :], writes=[r_c])
            P.dma("sp", abT[:], ada_bT[:, :, :], writes=[r_c])
            P.dma("sp", nm[:], nmixT[:, :, :], writes=[r_c])
            P.dma("sp", nf[:], nffnT[:, :, :], writes=[r_c])
            r_s = Res()
            P.op("act", lambda e: e.activation(out=sg[:], in_=cs[:], func=AF.Sigmoid), [r_c], [r_s])
            csf = sb(ph, "csf", [128, KC, 2], F32)
            P.op("dve", lambda e: e.tensor_tensor(out=csf[:], in0=cs[:], in1=sg[:], op=ALU.mult), [r_c, r_s], [r_s])

            def MM0(out, lhsT, rhs, start, stop, reads, writes):
                P.op("pe", lambda e: e.matmul(out, lhsT=lhsT, rhs=rhs, start=start, stop=stop), reads, writes)

            def TS0(out, in0, s1, reads, writes):
                P.op("dve", lambda e: e.tensor_scalar(out=out, in0=in0, scalar1=s1, scalar2=None, op0=ALU.add),
                     reads, writes)
            for l in range(2):
                for cb in range(24):
                    i = stctr[0] % NST
                    stctr[0] += 1
                    st = wstage[i][:, 0:KC * 256].rearrange("p (k n) -> p k n", n=256)
                    q = ("sp", "act")[qctr[0] % 2]
                    qctr[0] += 1
                    P.dma(q, st, ada_w[l, :, cb * 256:(cb + 1) * 256].rearrange("(k p) n -> p k n", p=128),
                          writes=[wstres[i]])
                    pb, pr = pbank[cb % 2], pres[cb % 2]
                    for m in range(2):
                        for kc in range(KC):
                            MM0(pb[:, 2 * m:2 * m + 2], st[:, kc, m * 128:(m + 1) * 128], csf[:, kc, :],
                                kc == 0, kc == KC - 1, [wstres[i], r_s], [pr])
                    for m in range(2):
                        ecol = cb * 2 + m
                        TS0(modv[:, l, ecol, :], pb[:, 2 * m:2 * m + 2], abT[:, l, ecol:ecol + 1], [pr, r_c], [r_mod])
                for mi, (wh, ng) in enumerate(((1, nm), (4, nf))):
                    for w in range(2):
                        P.op("dve", lambda e, l=l, mi=mi, wh=wh, ng=ng, w=w: e.scalar_tensor_tensor(
                            out=gm[:, l, mi, w, :], in0=modv[:, l, wh * 8:(wh + 1) * 8, w], scalar=1.0,
                            in1=ng[:, l, :], op0=ALU.add, op1=ALU.mult), [r_mod, r_c], [r_mod])
        P.barrier()

    def mvec(l, which, w):
        return modv[:, l, which * 8:(which + 1) * 8, w]

    def norm_mod(ph_scr, xT, xres, t0, n, gmv, shv, outT, ot0, outres, extra_reads=()):
        sq, rstd, tmp, r_sq, r_rstd, r_tmp = ph_scr
        for kc in range(KC):
            eng = "act" if kc % 2 == 0 else "pool"
            if eng == "act":
                P.op("act", lambda e, kc=kc: e.activation(out=sq[:, kc, :n], in_=xT[:, kc, t0:t0 + n], func=AF.Square),
                     [xres], [r_sq[kc]])
            else:
                P.op("pool", lambda e, kc=kc: e.tensor_tensor(out=sq[:, kc, :n], in0=xT[:, kc, t0:t0 + n],
                                                              in1=xT[:, kc, t0:t0 + n], op=ALU.mult), [xres], [r_sq[kc]])
        pb, pr = pbank[6], pres[6]
        for kc in range(KC):
            P.op("pe", lambda e, kc=kc: e.matmul(pb[:, :n], lhsT=onesB, rhs=sq[:, kc, :n], start=(kc == 0),
                                                 stop=(kc == KC - 1)), [r_sq[kc], r_const], [pr])
        P.op("act", lambda e: e.activation(out=rstd[:, :n], in_=pb[:, :n], func=AF.Ln, scale=1.0 / D,
                                           bias=epsT[:, 0:1]), [pr, r_const], [r_rstd])
        P.op("act", lambda e: e.activation(out=rstd[:, :n], in_=rstd[:, :n], func=AF.Exp, scale=-0.5),
             [r_rstd], [r_rstd])
        for kc in range(KC):
            P.op("dve", lambda e, kc=kc: e.scalar_tensor_tensor(
                out=tmp[:, kc % 2, :n], in0=xT[:, kc, t0:t0 + n], scalar=gmv[:, kc:kc + 1], in1=rstd[:, :n],
                op0=ALU.mult, op1=ALU.mult), [xres, r_rstd, r_mod] + list(extra_reads), [r_tmp[kc % 2]])
            if shv is not None:
                P.op("act", lambda e, kc=kc: e.activation(out=outT[:, kc, ot0:ot0 + n], in_=tmp[:, kc % 2, :n],
                                                          func=AF.Identity, bias=shv[:, kc:kc + 1], scale=1.0),
                     [r_tmp[kc % 2], r_mod], [outres])
            else:
                P.op("act", lambda e, kc=kc: e.activation(out=outT[:, kc, ot0:ot0 + n], in_=tmp[:, kc % 2, :n],
                                                          func=AF.Copy, scale=1.0), [r_tmp[kc % 2]], [outres])

    def mk_norm_scr(ph):
        sq = sb(ph, "n_sq", [128, KC, 512], BF16)
        rstd = sb(ph, "n_rstd", [128, 512], F32)
        tmp = sb(ph, "n_tmp", [128, 2, 512], F32)
        return (sq, rstd, tmp, [Res() for _ in range(KC)], Res(), [Res(), Res()])

    mmrr = [0]

    def next_bank():
        i = mmrr[0] % 6
        mmrr[0] += 1
        return pbank[i], pres[i]

    def linear_fm(w_ap, kcn, ncols, inT, in_res, T, epilogue, col0=0):
        nblk = (ncols + 511) // 512
        per = WEL // (kcn * 128)
        per = min(per, 4)
        blocks = []
        c = 0
        while c < ncols:
            n = min(per * 128, ncols - c)
            blocks.append((c, n))
            c += n
        def mk(b):
            c, n = b
            return [(w_ap[:, col0 + c:col0 + c + n], kcn, 0, n)]
        loaded = [wload(mk(blocks[0]))]
        for bi, (c, n) in enumerate(blocks):
            if bi + 1 < len(blocks):
                loaded.append(wload(mk(blocks[bi + 1])))
            wt, wr = loaded[bi]
            wv = wview(wt, kcn, n)
            for t0 in range(0, T, 512):
                tn = min(512, T - t0)
                for m in range(n // 128):
                    pb, pr = next_bank()
                    for kc in range(kcn):
                        P.op("pe", lambda e, pb=pb, wv=wv, m=m, kc=kc, t0=t0, tn=tn: e.matmul(
                            pb[:, :tn], lhsT=wv[:, kc, m * 128:(m + 1) * 128], rhs=inT[:, kc, t0:t0 + tn],
                            start=(kc == 0), stop=(kc == kcn - 1)), [wr, in_res], [pr])
                    epilogue((c // 128) + m, t0, tn, pb, pr)

    def residual_epilogue(xT, xres, gv, xoff=0):
        def ep(mc, t0, tn, pb, pr):
            P.op("dve", lambda e: e.scalar_tensor_tensor(
                out=xT[:, mc, xoff + t0:xoff + t0 + tn], in0=pb[:, :tn], scalar=gv[:, mc:mc + 1],
                in1=xT[:, mc, xoff + t0:xoff + t0 + tn], op0=ALU.mult, op1=ALU.add), [pr, r_mod, xres], [xres])
        return ep

    def ffn(ph, l, hT, hres, T, xT, xres, gv, TILE=512):
        TILE = min(TILE, T)
        hid = sb(ph, "hid", [128, FC, TILE], BF16)
        sil = sb(ph, "sil", [128, 2, 512], F32)
        r_hid = Res()
        r_sil = [Res(), Res()]
        sctr = [0]

        def emitA(pa, pra, pb_, prb, w1v, w3v, wr, m, fc, ht0, hn, off):
            for kc in range(KC):
                MM(pa[:, :hn], w1v[:, kc, m * 128:(m + 1) * 128], hT[:, kc, ht0:ht0 + hn], kc == 0, kc == KC - 1,
                   [wr, hres], [pra])
            for kc in range(KC):
                MM(pb_[:, :hn], w3v[:, kc, m * 128:(m + 1) * 128], hT[:, kc, ht0:ht0 + hn], kc == 0, kc == KC - 1,
                   [wr, hres], [prb])
            si = sctr[0] % 2
            sctr[0] += 1
            ACT(sil[:, si, :hn], pa[:, :hn], AF.Silu, [pra], [r_sil[si]])
            TT("dve", hid[:, fc, off:off + hn], sil[:, si, :hn], pb_[:, :hn], ALU.mult, [prb, r_sil[si]], [r_hid])

        def emitB(pb, pr, wv, wr, m, mc, ht0, hn, off):
            for fc in range(FC):
                MM(pb[:, :hn], wv[:, fc, m * 128:(m + 1) * 128], hid[:, fc, off:off + hn], fc == 0, fc == FC - 1,
                   [wr, r_hid], [pr])
            STT(xT[:, mc, ht0:ht0 + hn], pb[:, :hn], gv[:, mc:mc + 1], xT[:, mc, ht0:ht0 + hn], ALU.mult, ALU.add,
                [pr, r_mod, xres], [xres])

        for t0 in range(0, T, TILE):
            tn_all = min(TILE, T - t0)
            halves = [(t0 + o, min(512, tn_all - o), o) for o in range(0, tn_all, 512)]
            blocks = [(c, min(256, DFF - c)) for c in range(0, DFF, 256)]

            def mk(b_):
                c, n = b_
                return [(ffn_w1[l, :, c:c + n], KC, 0, n), (ffn_w3[l, :, c:c + n], KC, KC * 256, n)]
            loaded = [wload(mk(blocks[0]))]
            for bi, (c, n) in enumerate(blocks):
                if bi + 1 < len(blocks):
                    loaded.append(wload(mk(blocks[bi + 1])))
                wt, wr = loaded[bi]
                w1v = wview(wt, KC, n, 0)
                w3v = wview(wt, KC, n, KC * 256)
                for m in range(n // 128):
                    for ht0, hn, off in halves:
                        pa, pra = next_bank()
                        pb_, prb = next_bank()
                        emitA(pa, pra, pb_, prb, w1v, w3v, wr, m, c // 128 + m, ht0, hn, off)
            blocks = [(c, 128) for c in range(0, D, 128)]

            def mk2(b_):
                c, n = b_
                return [(ffn_w2[l, :, c:c + n], FC, 0, n)]
            loaded = [wload(mk2(blocks[0]))]
            for bi, (c, n) in enumerate(blocks):
                if bi + 1 < len(blocks):
                    loaded.append(wload(mk2(blocks[bi + 1])))
                wt, wr = loaded[bi]
                wv = wview(wt, FC, n)
                for m in range(n // 128):
                    for ht0, hn, off in halves:
                        pb, pr = next_bank()
                        emitB(pb, pr, wv, wr, m, c // 128 + m, ht0, hn, off)

    def MM(out, lhsT, rhs, start, stop, reads, writes):
        P.op("pe", lambda e: e.matmul(out, lhsT=lhsT, rhs=rhs, start=start, stop=stop), reads, writes)

    def TR(out, in_, ident, reads, writes):
        P.op("pe", lambda e: e.transpose(out, in_, ident), reads, writes)

    def ACT(out, in_, func, reads, writes, bias=None, scale=None, accum=None):
        kw = {}
        if bias is not None:
            kw["bias"] = bias
        if scale is not None:
            kw["scale"] = scale
        if accum is not None:
            kw["accum_out"] = accum
        P.op("act", lambda e: e.activation(out=out, in_=in_, func=func, **kw), reads, writes)

    def TT(eng, out, in0, in1, op, reads, writes):
        P.op(eng, lambda e: e.tensor_tensor(out=out, in0=in0, in1=in1, op=op), reads, writes)

    def TS(eng, out, in0, s1, op0, reads, writes, s2=None, op1=None):
        if op1 is None:
            P.op(eng, lambda e: e.tensor_scalar(out=out, in0=in0, scalar1=s1, scalar2=None, op0=op0), reads, writes)
        else:
            P.op(eng, lambda e: e.tensor_scalar(out=out, in0=in0, scalar1=s1, scalar2=s2, op0=op0, op1=op1),
                 reads, writes)

    def STT(out, in0, scalar, in1, op0, op1, reads, writes):
        P.op("dve", lambda e: e.scalar_tensor_tensor(out=out, in0=in0, scalar=scalar, in1=in1, op0=op0, op1=op1),
             reads, writes)

    def CP(eng, out, in_, reads, writes):
        if eng == "act":
            P.op("act", lambda e: e.activation(out=out, in_=in_, func=AF.Copy), reads, writes)
        else:
            P.op(eng, lambda e: e.tensor_copy(out=out, in_=in_), reads, writes)

    def MSET(eng, ap, val, writes):
        P.op(eng, lambda e: e.memset(ap, val), [], writes)

    def RECIP(out, in_, reads, writes):
        P.op("dve", lambda e: e.reciprocal(out=out, in_=in_), reads, writes)

    def RMAX(out, in_, reads, writes):
        P.op("dve", lambda e: e.reduce_max(out=out, in_=in_, axis=AX.X), reads, writes)

    def RSUM(out, in_, reads, writes):
        P.op("dve", lambda e: e.reduce_sum(out=out, in_=in_, axis=AX.X), reads, writes)

    ONE = epsT[:, 1:2]
    EPSC = epsT[:, 0:1]

    def round_robin(gens):
        gens = list(gens)
        while gens:
            for g in list(gens):
                try:
                    next(g)
                except StopIteration:
                    gens.remove(g)

    lbT = sb(G, "lbT", [128, 512], F32)
    omlT = sb(G, "omlT", [128, 512], F32)
    mlgT = sb(G, "mlgT", [128, 512], F32)
    hggT = sb(G, "hggT", [128, 512], F32)
    gbT = sb(G, "gbT", [128, 16], F32)
    dagS = sb(G, "dagS", [128, 128], F32)
    lamv = sb(G, "lamv", [128, 4], F32)
    r_par = Res()

    def phase_params():
        with phase() as ph:
            lbl = sb(ph, "lbl", [128, 2, 512], F32)
            lml = sb(ph, "lml", [128, 4, 64], F32)
            tmp = sb(ph, "ptmp", [128, 2, 64], F32)
            r_l = Res()
            P.dma("sp", lbl[:], lb_log[:, :, :], writes=[r_l])
            P.dma("sp", lml[:], od_lam[:, :, :], writes=[r_l])
            P.dma("sp", mlgT[:], mlg_g[:, :], writes=[r_par])
            P.dma("sp", hggT[:], hgg_g[:, :], writes=[r_par])
            P.dma("sp", gbT[:], gate_b[:, :], writes=[r_par])
            P.dma("sp", dagS[:], da_g[:, :], writes=[r_par])
            TT("dve", lbT[:], lbl[:, 0, :], lbl[:, 1, :], ALU.subtract, [r_l], [r_par])
            ACT(lbT[:], lbT[:], AF.Sigmoid, [r_par], [r_par])
            TS("dve", omlT[:], lbT[:], -1.0, ALU.mult, [r_par], [r_par], s2=1.0, op1=ALU.add)
            TS("dve", dagS[:], dagS[:], float(1.0 - LAM_INIT1), ALU.mult, [r_par], [r_par])
            TT("dve", tmp[:, 0, :], lml[:, 0, :], lml[:, 1, :], ALU.mult, [r_l], [r_l])
            TT("dve", tmp[:, 1, :], lml[:, 2, :], lml[:, 3, :], ALU.mult, [r_l], [r_l])
            RSUM(lamv[:, 2:3], tmp[:, 0, :], [r_l], [r_par])
            RSUM(lamv[:, 3:4], tmp[:, 1, :], [r_l], [r_par])
            ACT(lamv[:, 2:4], lamv[:, 2:4], AF.Exp, [r_par], [r_par])
            TT("dve", lamv[:, 0:1], lamv[:, 2:3], lamv[:, 3:4], ALU.subtract, [r_par], [r_par])
            TS("dve", lamv[:, 0:1], lamv[:, 0:1], float(LAM_INIT1), ALU.add, [r_par], [r_par])
            TS("dve", lamv[:, 1:2], lamv[:, 0:1], -1.0, ALU.mult, [r_par], [r_par])

    SM = pbank[6]
    SMr = pres[6]

    def head_norm_out(scr, src_ap, gain_ap, gate_ap, dst_ap, reads, dres):
        junk, ss, t1, t2, r_s = scr
        ACT(junk[:], src_ap, AF.Square, reads, [r_s], accum=ss[:, 0:1])
        ACT(ss[:, 1:2], ss[:, 0:1], AF.Ln, [r_s, r_const], [r_s], scale=1.0 / 128, bias=EPSC)
        ACT(ss[:, 1:2], ss[:, 1:2], AF.Exp, [r_s], [r_s], scale=-0.5)
        STT(t1[:], src_ap, ss[:, 1:2], gain_ap, ALU.mult, ALU.mult, reads + [r_s, r_par], [r_s])
        if gate_ap is not None:
            TT("pool", t2[:], t1[:], gate_ap, ALU.mult, reads + [r_s], [r_s])
        else:
            CP("pool", t2[:], t1[:], [r_s], [r_s])
        TR(pbf[:, 0:128], t2[:], identB, [r_s, r_const], [pbf_res[0]])
        CP("act", dst_ap, pbf[:, 0:128], [pbf_res[0]], [dres])

    HNG = 8

    def mk_hnb(st):
        return {"junk": sb(st, "hb_j", [128, 128], F32), "ss": sb(st, "hb_ss", [128, 2 * HNG], F32),
                "t1": sb(st, "hb_t1", [128, HNG, 128], F32), "t2": sb(st, "hb_t2", [128, HNG, 128], BF16),
                "rj": Res(), "rss": Res(), "r1": [Res() for _ in range(HNG)], "r2": [Res() for _ in range(HNG)]}

    def head_norm_batch(hb, items):
        for g0 in range(0, len(items), HNG):
            grp = items[g0:g0 + HNG]
            n = len(grp)
            ss = hb["ss"]
            for k, (src_ap, gain_ap, gate_ap, dst_ap, reads, dres) in enumerate(grp):
                ACT(hb["junk"][:], src_ap, AF.Square, reads, [hb["rj"], hb["rss"]], accum=ss[:, k:k + 1])
            ACT(ss[:, HNG:HNG + n], ss[:, 0:n], AF.Ln, [hb["rss"], r_const], [hb["rss"]], scale=1.0 / 128, bias=EPSC)
            ACT(ss[:, HNG:HNG + n], ss[:, HNG:HNG + n], AF.Exp, [hb["rss"]], [hb["rss"]], scale=-0.5)
            for k, (src_ap, gain_ap, gate_ap, dst_ap, reads, dres) in enumerate(grp):
                STT(hb["t1"][:, k, :], src_ap, ss[:, HNG + k:HNG + k + 1], gain_ap, ALU.mult, ALU.mult,
                    reads + [hb["rss"], r_par], [hb["r1"][k]])
            for k, (src_ap, gain_ap, gate_ap, dst_ap, reads, dres) in enumerate(grp):
                if gate_ap is not None:
                    TT("pool", hb["t2"][:, k, :], hb["t1"][:, k, :], gate_ap, ALU.mult, reads + [hb["r1"][k]], [hb["r2"][k]])
                else:
                    CP("pool", hb["t2"][:, k, :], hb["t1"][:, k, :], [hb["r1"][k]], [hb["r2"][k]])
            for k in range(n):
                TR(pbf[:, k * 128:(k + 1) * 128], hb["t2"][:, k, :], identB, [hb["r2"][k], r_const], [pbf_res[0]])
            for k, (src_ap, gain_ap, gate_ap, dst_ap, reads, dres) in enumerate(grp):
                CP("act" if k % 2 == 0 else "dve", dst_ap, pbf[:, k * 128:(k + 1) * 128], [pbf_res[0]], [dres])

    def mk_hn_scr(st):
        return (sb(st, "hn_j", [128, 128], F32), sb(st, "hn_ss", [128, 2], F32), sb(st, "hn_t1", [128, 128], F32),
                sb(st, "hn_t2", [128, 128], BF16), Res())

    def even_mixer(seg, hT, hres, moT, mores, nseq, S, is_prompt, SP=None):
        SP = SP or S
        NT = S // 128
        NTp = SP // 128
        nparts = S // SP
        with phase() as ph:
            nsets = 1 if nparts == 1 else 2
            PB = []
            for s_ in range(nsets):
                bs = {}
                bs["qTm"] = sb(ph, "qTm", [128, SP], BF16)
                bs["kTm"] = sb(ph, "kTm", [128, SP], BF16)
                bs["qTh"] = sb(ph, "qTh", [128, SP], BF16)
                bs["ktm"] = sb(ph, "ktm", [128, NTp, 128], BF16)
                bs["v1"] = sb(ph, "v1", [128, NTp, 130], BF16)
                bs["g16"] = sb(ph, "g16", [128, NTp, 16], F32)
                bs["lp"] = sb(ph, "lp", [128, NTp, 8], F32)
                bs["Bt"] = sb(ph, "Bt", [128, NTp, 4], F32)
                bs["at"] = sb(ph, "at", [128, NTp, 2], F32)
                bs["amx"] = sb(ph, "amx", [128, NTp, 2], F32)
                bs["lfall"] = sb(ph, "lfall", [128, NTp, 2, 128], F32)
                bs["kcall"] = sb(ph, "kcall", [128, NTp, 2, 128], BF16)
                bs["itm"] = sb(ph, "itm", [128, NTp, 128], BF16)
                bs["r_proj"], bs["r_gate"], bs["r_v1"], bs["r_lf"] = Res(), Res(), Res(), Res()
                for i in range(NTp):
                    MSET("pool", bs["v1"][:, i, 128:130], 1.0, [bs["r_v1"]])
                PB.append(bs)
            sgo = sb(ph, "sgo", [128, NT, 128], BF16)
            ohg = sb(ph, "ohg", [128, NT, 128], BF16)
            hsum = sb(ph, "hsum", [128, NT, 128], F32)
            osum = sb(ph, "osum", [128, NT, 128], F32)
            scrA = sb(ph, "scrA", [128, 256], F32)
            e1 = sb(ph, "e1", [128, 8, 8], F32)
            sm2 = sb(ph, "sm2", [128, 24], F32)
            hn = mk_hnb(ph)
            r_scr, r_go = Res(), Res()
            r_hs = [Res() for _ in range(NT)]
            r_os = [Res() for _ in range(NT)]
            ch = []
            for c in range(4):
                d = {}
                d["st"] = sb(ph, "st%d" % c, [128, 130], F32)
                d["m"] = sb(ph, "m%d" % c, [128, 8], F32)
                d["b1"] = sb(ph, "b1%d" % c, [128, 130], BF16)
                d["b2"] = sb(ph, "b2%d" % c, [128, 130], BF16)
                d["b3"] = sb(ph, "b3%d" % c, [128, 128], BF16)
                d["b4"] = sb(ph, "b4%d" % c, [128, 128], BF16)
                d["f1"] = sb(ph, "f1%d" % c, [128, 128], F32)
                d["f2"] = sb(ph, "f2%d" % c, [128, 128], F32)
                d["e4"] = sb(ph, "e4%d" % c, [128, 4], F32)
                d["q4"] = sb(ph, "q4%d" % c, [128, 4, 128], BF16)
                d["k4"] = sb(ph, "k4%d" % c, [128, 4, 128], BF16)
                d["b5"] = sb(ph, "b5%d" % c, [128, 2, 128], BF16)
                d["r"] = Res()
                d["rs"] = Res()
                d["R"] = {k: Res() for k in ("m0", "G", "nG", "u", "wi", "fl", "dn", "rc", "b1", "b2", "b3", "b4", "f1", "f2",
                                             "e4", "q4", "k4", "st", "sb0", "sb1")}
                ch.append(d)
            pbh = [pbf[:, 128:256], pbf[:, 256:384]]
            pbhr = [pbf_res[0], pbf_res[0]]
            MLP = []
            HGP = []
            for d_ in range(2):
                bk = pbank[d_]
                rb = Res(True)
                MLP.append({"ST": bk[:, 0:128], "ND": bk[:, 128:257], "dCN": bk[:, 257:386],
                            "rST": rb, "rND": rb, "rdCN": rb})
                ba, bb = pbank[2 + 2 * d_], pbank[3 + 2 * d_]
                ra, rbb = Res(True), Res(True)
                HGP.append({"D1": ba[:, 0:128], "D1T": ba[:, 128:256], "sc": ba[:, 256:384], "O": ba[:, 384:512],
                            "dS": [bb[:, 0:128], bb[:, 128:256]], "al": bb[:, 256:260],
                            "rD1": ra, "rD1T": ra, "rsc": ra, "rO": ra,
                            "rdS": [rbb, rbb], "ral": rbb})

            def chain_bank_fence(after):
                regs = []
                for d_ in range(2):
                    regs += [MLP[d_]["rST"], MLP[d_]["rND"], MLP[d_]["rdCN"], HGP[d_]["rD1"], HGP[d_]["rD1T"], HGP[d_]["rsc"],
                             HGP[d_]["rO"], HGP[d_]["rdS"][0], HGP[d_]["rdS"][1], HGP[d_]["ral"]]
                banks = [pres[i] for i in range(6)]
                if after:
                    P.fence("pe", regs, banks)
                else:
                    P.fence("pe", banks, regs)

            wsets = [[(wring[i], wres[i]) for i in range(3)]]
            if is_prompt:
                wsets.append([(sb(ph, "wx%d" % i, [128, WEL], BF16), Res()) for i in range(3)])

            def load_hg(h):
                (wf, wfr), (wa, war), (wb, wbr) = wsets[h % len(wsets)]
                wfv = wview(wf, KC, 384)
                wav = wview(wa, KC, 400)
                wbv = wview(wb, KC, 512)

                def wl(dst, off, n, wr_):
                    wpart(dst, ev_w_in[:, off:off + n], KC, n, wr_)
                wl(wfv[:, :, 0:128], OFF_MLQ + h * 128, 128, wfr)
                wl(wfv[:, :, 128:256], OFF_MLK + h * 128, 128, wfr)
                wl(wfv[:, :, 256:384], OFF_HGQ + h * 128, 128, wfr)
                wl(wav[:, :, 0:128], OFF_MLK + h * 128, 128, war)
                wl(wav[:, :, 128:256], OFF_MLV + h * 128, 128, war)
                wl(wav[:, :, 256:384], OFF_MLO + h * 128, 128, war)
                wl(wav[:, :, 384:400], OFF_MLG, 16, war)
                wl(wbv[:, :, 0:128], OFF_HGFF + h * 128, 128, wbr)
                wl(wbv[:, :, 128:256], OFF_HGFB + h * 128, 128, wbr)
                wl(wbv[:, :, 256:384], OFF_HGI + h * 128, 128, wbr)
                wl(wbv[:, :, 384:512], OFF_HGO + h * 128, 128, wbr)
                return wfv, wav, wbv, wfr, war, wbr

            wctr[0] += 3
            loaded_hg = {0: load_hg(0)}
            for h in range(4):
                if h not in loaded_hg:
                    loaded_hg[h] = load_hg(h)
                wfv, wav, wbv, wfr, war, wbr = loaded_hg.pop(h)
                if len(wsets) > 1 and h + 1 < 4:
                    loaded_hg[h + 1] = load_hg(h + 1)
                hs = slice(h * 128, (h + 1) * 128)

                def inproj(T0, gt0, bs, h=h, hs=hs, wfv=wfv, wav=wav, wbv=wbv, wfr=wfr, war=war, wbr=wbr):
                    qTm, kTm, qTh, ktm, v1, g16, lp, Bt, at, amx, lfall, kcall, itm = (bs[k] for k in (
                        "qTm", "kTm", "qTh", "ktm", "v1", "g16", "lp", "Bt", "at", "amx", "lfall", "kcall", "itm"))
                    r_proj, r_gate, r_v1 = bs["r_proj"], bs["r_gate"], bs["r_v1"]
                    r_lf = bs["r_lf"]
                    for j, (dst, scl) in enumerate(((qTm, 128 ** -0.5), (kTm, 1.0), (qTh, 1.0))):
                        for tb in range(0, SP, 512):
                            n = min(512, SP - tb)
                            pb, pr = next_bank()
                            for kc in range(KC):
                                MM(pb[:, :n], wfv[:, kc, j * 128:(j + 1) * 128], hT[:, kc, T0 + tb:T0 + tb + n],
                                   kc == 0, kc == KC - 1, [wfr, hres], [pr])
                            ACT(dst[:, tb:tb + n], pb[:, :n], AF.Copy, [pr], [r_proj], scale=float(scl))
                    for i in range(NTp):
                        tk = slice(T0 + i * 128, T0 + (i + 1) * 128)
                        pb, pr = next_bank()
                        for kc in range(KC):
                            MM(pb[:, 0:400], hT[:, kc, tk], wav[:, kc, 0:400], kc == 0, kc == KC - 1, [war, hres], [pr])
                        CP("dve", ktm[:, i, :], pb[:, 0:128], [pr], [r_proj])
                        CP("dve", v1[:, i, 0:128], pb[:, 128:256], [pr], [r_proj, r_v1])
                        CP("act", sgo[:, gt0 + i, :], pb[:, 256:384], [pr], [r_go])
                        TT("dve", g16[:, i, :], pb[:, 384:400], gbT[:], ALU.add, [pr, r_par], [r_gate])
                        pb, pr = next_bank()
                        for kc in range(KC):
                            MM(pb[:, 0:512], hT[:, kc, tk], wbv[:, kc, 0:512], kc == 0, kc == KC - 1, [wbr, hres], [pr])
                        CP("act", lfall[:, i, :, :], pb[:, 0:256].rearrange("p (a b) -> p a b", a=2), [pr], [r_lf])
                        CP("dve", itm[:, i, :], pb[:, 256:384], [pr], [r_proj])
                        CP("act", ohg[:, gt0 + i, :], pb[:, 384:512], [pr], [r_go])
                    ACT(sgo[:, gt0:gt0 + NTp, :], sgo[:, gt0:gt0 + NTp, :], AF.Sigmoid, [r_go], [r_go])
                    ACT(lfall[:, :, :, :], lfall[:, :, :, :], AF.Sigmoid, [r_lf], [r_lf])
                    for i in range(NTp):
                        for dd in range(2):
                            TT("dve", lfall[:, i, dd, :], lfall[:, i, dd, :], omlT[:, hs], ALU.mult, [r_lf, r_par], [r_lf])
                            TT("dve", lfall[:, i, dd, :], lfall[:, i, dd, :], lbT[:, hs], ALU.add, [r_lf, r_par], [r_lf])
                    TS("dve", kcall[:, :, :, :], lfall[:, :, :, :], -1.0, ALU.mult, [r_lf], [r_proj], s2=1.0, op1=ALU.add)
                    ACT(lfall[:, :, :, :], lfall[:, :, :, :], AF.Ln, [r_lf, r_proj], [r_lf, r_proj])
                    ACT(ohg[:, gt0:gt0 + NTp, :], ohg[:, gt0:gt0 + NTp, :], AF.Silu, [r_go], [r_go])
                    ACT(e1[:, 0:NTp, :], g16[:, :, 8:16], AF.Exp, [r_gate], [r_scr], scale=-1.0)
                    ACT(lp[:, :, :], e1[:, 0:NTp, :], AF.Ln, [r_scr, r_const], [r_gate], bias=ONE)
                    MM(SM[:, 0:NTp], triF, lp[:, :, h], True, True, [r_gate, r_const], [SMr])
                    MM(SM[:, NTp:2 * NTp], triB, lp[:, :, 4 + h], True, True, [r_gate, r_const], [SMr])
                    MM(SM[:, 2 * NTp:3 * NTp], onesF, lp[:, :, h], True, True, [r_gate, r_const], [SMr])
                    MM(SM[:, 3 * NTp:4 * NTp], onesF, lp[:, :, 4 + h], True, True, [r_gate, r_const], [SMr])
                    CP("dve", Bt[:, :, :], SM[:, 0:4 * NTp].rearrange("p (b a) -> p a b", b=4), [SMr], [r_gate])
                    TT("dve", at[:, :, 0:1], g16[:, :, h:h + 1], Bt[:, :, 0:1], ALU.add, [r_gate], [r_gate])
                    TT("dve", at[:, :, 1:2], g16[:, :, 4 + h:5 + h], Bt[:, :, 1:2], ALU.add, [r_gate], [r_gate])
                    n2 = 2 * NTp
                    TR(SM[0:n2, 128:256], at[:, :, :].rearrange("p a b -> p (a b)"), identF, [r_gate, r_const], [SMr])
                    RMAX(sm2[0:n2, 0:1], SM[0:n2, 128:256], [SMr], [r_scr])
                    TS("dve", sm2[0:n2, 4:4 + n2], identF[0:n2, 0:n2], sm2[0:n2, 0:1], ALU.mult, [r_scr, r_const], [r_scr])
                    MM(SM[:, 64:64 + n2], onesF[0:n2, :], sm2[0:n2, 4:4 + n2], True, True, [r_scr, r_const], [SMr])
                    CP("dve", amx[:, :, :], SM[:, 64:64 + n2].rearrange("p (a b) -> p a b", b=2), [SMr], [r_gate])

                def ml_init(d, c, h=h):
                    cn, mm_ = c["st"], c["m"]
                    R = c["R"]
                    allm = [R[k] for k in ("m0", "G", "nG", "u", "wi", "fl", "dn", "rc")]
                    if is_prompt:
                        MSET("pool", cn[:], 0.0, [R["st"]])
                        MSET("pool", mm_[:], 0.0, allm)
                    else:
                        MSET("pool", mm_[:], 0.0, allm)
                        P.dma("sp", cn[:, 0:128], st_C[d, h, :, :], writes=[R["st"]])
                        P.dma("sp", cn[:, 128:129], st_n[d * 4 + h, :].rearrange("(p o) -> p o", o=1), writes=[R["st"]])
                        P.dma("sp", mm_[:, 0:1], st_m[d * 4 + h, :].rearrange("(p o) -> p o", o=1), writes=[R["m0"]])

                def ml_final(seq, d, c, h=h):
                    cn, mm_ = c["st"], c["m"]
                    P.dma("sp", o_C[seq, d, h, :, :], cn[:, 0:128], reads=[c["R"]["st"]], is_output=True)
                    P.dma("sp", o_n[seq, d, h, :].rearrange("(p o) -> p o", o=1), cn[:, 128:129], reads=[c["R"]["st"]],
                          is_output=True)
                    P.dma("sp", o_m[seq, d, h:h + 1].rearrange("(p o) -> p o", o=1), mm_[0:1, 0:1], reads=[c["R"]["m0"]],
                          is_output=True)

                def ml_run(d, c, tiles, bs):
                    cn, mm_ = c["st"], c["m"]
                    R = c["R"]
                    qTm, kTm, ktm, v1, Bt, at, amx = (bs[k] for k in ("qTm", "kTm", "ktm", "v1", "Bt", "at", "amx"))
                    r_proj, r_gate, r_v1 = bs["r_proj"], bs["r_gate"], bs["r_v1"]
                    mp = MLP[d]
                    mask = maskF if d == 0 else maskB
                    yield
                    for i, ig in tiles:
                        tk = slice(i * 128, (i + 1) * 128)
                        m0, Gt, nG, u, wi, fl, dn, rc = (mm_[:, k:k + 1] for k in range(8))
                        MM(mp["ST"], kTm[:, tk], qTm[:, tk], True, True, [r_proj], [mp["rST"]])
                        TT("dve", Gt, m0, amx[:, i, d:d + 1], ALU.max, [R["m0"], r_gate], [R["G"]])
                        yield
                        TS("dve", nG, Gt, -1.0, ALU.mult, [R["G"]], [R["nG"]])
                        yield
                        ACT(u, at[:, i, d:d + 1], AF.Exp, [r_gate, R["nG"]], [R["u"]], bias=nG)
                        ACT(wi, m0, AF.Exp, [R["m0"], R["nG"]], [R["wi"]], bias=nG)
                        ACT(fl, Bt[:, i, d:d + 1], AF.Exp, [r_gate, R["nG"]], [R["fl"]], bias=nG)
                        yield
                        TT("dve", m0, Gt, Bt[:, i, 2 + d:3 + d], ALU.subtract, [R["G"], r_gate], [R["m0"]])
                        STT(c["b3"][:], mp["ST"], u, mask, ALU.mult, ALU.mult, [mp["rST"], R["u"], r_const], [R["b3"]])
                        TS("dve", c["b4"][:], ktm[:, i, :], u, ALU.mult, [r_proj, R["u"]], [R["b4"]])
                        ACT(c["b1"][:, 0:129], cn[:, 0:129], AF.Copy, [R["st"], R["wi"]], [R["b1"]], scale=wi)
                        yield
                        MM(mp["ND"], c["b3"][:], v1[:, i, 0:129], True, False, [R["b3"], r_proj, r_v1], [mp["rND"]])
                        MM(mp["ND"], qTm[:, tk], c["b1"][:, 0:129], False, True, [R["b1"], r_proj], [mp["rND"]])
                        MM(mp["dCN"], c["b4"][:], v1[:, i, 0:129], True, True, [R["b4"], r_proj, r_v1], [mp["rdCN"]])
                        yield
                        TS("dve", dn, mp["ND"][:, 128:129], -1.0, ALU.mult, [mp["rND"]], [R["dn"]])
                        TT("dve", dn, dn, mp["ND"][:, 128:129], ALU.max, [mp["rND"], R["dn"]], [R["dn"]])
                        STT(cn[:, 0:129], cn[:, 0:129], wi, mp["dCN"], ALU.mult, ALU.add, [mp["rdCN"], R["wi"], R["st"]], [R["st"]])
                        yield
                        TT("dve", dn, dn, fl, ALU.max, [R["dn"], R["fl"]], [R["dn"]])
                        yield
                        RECIP(rc, dn, [R["dn"]], [R["rc"]])
                        yield
                        STT(hsum[:, ig, :], mp["ND"][:, 0:128], rc, hsum[:, ig, :], ALU.mult, ALU.add,
                            [mp["rND"], R["rc"], r_hs[ig]], [r_hs[ig]])
                        yield

                def hg_init(d, c, h=h):
                    S_ = c["st"]
                    MSET("pool", c["q4"][:], 0.0, [c["R"]["q4"]])
                    if is_prompt:
                        MSET("pool", S_[:], 0.0, [c["R"]["st"]])
                    else:
                        P.dma("sp", S_[:, 0:128], st_S[d, h, :, :], writes=[c["R"]["st"]])

                def hg_final(seq, d, c, h=h):
                    P.dma("sp", o_S[seq, d, h, :, :], c["st"][:, 0:128], reads=[c["R"]["st"]], is_output=True)

                def hg_run(d, c, tiles, bs):
                    S_ = c["st"]
                    R = c["R"]
                    qTh, lfall, kcall, itm = (bs[k] for k in ("qTh", "lfall", "kcall", "itm"))
                    r_proj = bs["r_proj"]
                    hp = HGP[d]
                    su = suF if d == 0 else suB
                    m32 = m32F if d == 0 else m32B
                    corder = range(4) if d == 0 else range(3, -1, -1)
                    yield
                    for i, ig in tiles:
                        tk = slice(i * 128, (i + 1) * 128)
                        lfd = lfall[:, i, d, :]
                        MM(hp["D1"], su, lfd, True, True, [r_proj, r_const], [hp["rD1"]])
                        MM(hp["D1T"], lfd, su, True, True, [r_proj, r_const], [hp["rD1T"]])
                        MM(hp["al"], lfd, CI, True, True, [r_proj, r_const], [hp["ral"]])
                        yield
                        ACT(c["f1"][:], hp["D1"], AF.Exp, [hp["rD1"]], [R["f1"]])
                        ACT(c["f2"][:], hp["D1T"], AF.Exp, [hp["rD1T"]], [R["f2"]], scale=-1.0)
                        ACT(c["e4"][:], hp["al"], AF.Exp, [hp["ral"]], [R["e4"]])
                        yield
                        TT("dve", c["b3"][:], kcall[:, i, d, :], c["f1"][:], ALU.mult, [R["f1"], r_proj], [R["b3"]])
                        TT("dve", c["b2"][:, 0:128], qTh[:, tk], c["f2"][:], ALU.mult, [R["f2"], r_proj], [R["b2"]])
                        yield
                        TR(pbh[d], c["b3"][:], identB, [R["b3"], r_const], [pbhr[d]])
                        for j in range(4):
                            cs_ = slice(32 * j, 32 * j + 32)
                            TT("dve", c["q4"][:, j, cs_], qTh[:, 128 * i + 32 * j:128 * i + 32 * j + 32], c["f2"][:, cs_],
                               ALU.mult, [R["f2"], r_proj], [R["q4"]])
                            TS("dve", c["k4"][:, j, :], c["b3"][:], CI[:, j:j + 1], ALU.mult, [R["b3"], r_const], [R["k4"]])
                        yield
                        CP("act", c["b4"][:], pbh[d], [pbhr[d]], [R["b4"]])
                        yield
                        MM(hp["sc"], c["b4"][:], c["b2"][:, 0:128], True, True, [R["b4"], R["b2"]], [hp["rsc"]])
                        yield
                        TT("dve", c["b1"][:, 0:128], hp["sc"], m32, ALU.mult, [hp["rsc"], r_const], [R["b1"]])
                        yield
                        MM(hp["O"], c["b1"][:, 0:128], itm[:, i, :], True, False, [R["b1"], r_proj], [hp["rO"]])
                        for jn, j in enumerate(corder):
                            sbi = jn % 2
                            sbuf_, sbr_ = c["b5"][:, sbi, :], R["sb%d" % sbi]
                            ACT(sbuf_, S_[:, 0:128], AF.Copy, [R["st"], R["e4"]], [sbr_], scale=c["e4"][:, j:j + 1])
                            MM(hp["dS"][sbi], c["k4"][:, j, :], itm[:, i, :], True, True, [R["k4"], r_proj], [hp["rdS"][sbi]])
                            yield
                            MM(hp["O"], c["q4"][:, j, :], sbuf_, False, jn == 3, [R["q4"], sbr_], [hp["rO"]])
                            STT(S_[:, 0:128], S_[:, 0:128], c["e4"][:, j:j + 1], hp["dS"][sbi], ALU.mult, ALU.add,
                                [hp["rdS"][sbi], R["e4"], R["st"]], [R["st"]])
                            yield
                        TT("dve", osum[:, ig, :], osum[:, ig, :], hp["O"], ALU.add, [hp["rO"], r_os[ig]], [r_os[ig]])
                        yield

                for seq in range(nseq):
                    T0 = seq * S
                    MSET("pool", hsum[:], 0.0, r_hs)
                    MSET("pool", osum[:], 0.0, r_os)
                    for d in range(2):
                        ml_init(d, ch[d])
                        hg_init(d, ch[2 + d])
                    if nparts == 1:
                        inproj(T0, 0, PB[0])
                        fw = [(i, i) for i in range(NT)]
                        bw = [(i, i) for i in range(NT - 1, -1, -1)]
                        chain_bank_fence(False)
                        round_robin([ml_run(0, ch[0], fw, PB[0]), hg_run(0, ch[2], fw, PB[0]),
                                     ml_run(1, ch[1], bw, PB[0]), hg_run(1, ch[3], bw, PB[0])])
                        chain_bank_fence(True)
                    else:
                        if SEQ_PASSES:
                            for p_ in range(nparts):
                                inproj(T0 + p_ * SP, p_ * NTp, PB[0])
                                tl = [(i, p_ * NTp + i) for i in range(NTp)]
                                chain_bank_fence(False)
                                round_robin([ml_run(0, ch[0], tl, PB[0]), hg_run(0, ch[2], tl, PB[0])])
                                chain_bank_fence(True)
                            for p_ in range(nparts - 1, -1, -1):
                                inproj(T0 + p_ * SP, p_ * NTp, PB[0])
                                tl = [(i, p_ * NTp + i) for i in range(NTp - 1, -1, -1)]
                                chain_bank_fence(False)
                                round_robin([ml_run(1, ch[1], tl, PB[0]), hg_run(1, ch[3], tl, PB[0])])
                                chain_bank_fence(True)
                        for st_ in (range(nparts) if not SEQ_PASSES else ()):
                            pf, pb_ = st_, nparts - 1 - st_
                            inproj(T0 + pf * SP, pf * NTp, PB[0])
                            inproj(T0 + pb_ * SP, pb_ * NTp, PB[1])
                            tf = [(i, pf * NTp + i) for i in range(NTp)]
                            tb_ = [(i, pb_ * NTp + i) for i in range(NTp - 1, -1, -1)]
                            chain_bank_fence(False)
                            round_robin([ml_run(0, ch[0], tf, PB[0]), hg_run(0, ch[2], tf, PB[0]),
                                         ml_run(1, ch[1], tb_, PB[1]), hg_run(1, ch[3], tb_, PB[1])])
                            chain_bank_fence(True)
                    if is_prompt:
                        for d in range(2):
                            ml_final(seq, d, ch[d])
                            hg_final(seq, d, ch[2 + d])
                    items = []
                    for i in range(NT):
                        tk = slice(T0 + i * 128, T0 + (i + 1) * 128)
                        items.append((hsum[:, i, :], mlgT[:, hs], sgo[:, i, :], moT[:, h, tk], [r_hs[i], r_go], mores))
                        items.append((osum[:, i, :], hggT[:, hs], ohg[:, i, :], moT[:, 4 + h, tk], [r_os[i], r_go], mores))
                    head_norm_batch(hn, items)

    def odd_mixer(seg, hT, hres, moT, mores, nseq, S, is_prompt, hTq=None, hqres=None, SQ=None):
        NT = S // 128
        rope = not is_prompt
        NKT = NT if is_prompt else NT + PAST // 128
        koff = 0 if is_prompt else PAST // 128
        QN = S if hTq is None else SQ
        with phase() as ph:
            qT = sb(ph, "a_qT", [128, nseq * QN], BF16)
            qT2 = sb(ph, "a_qT2", [128, nseq * QN], BF16)
            kT = sb(ph, "a_kT", [128, nseq * NKT * 128], BF16)
            v1h = sb(ph, "a_v1", [128, nseq * NKT, 130], BF16)
            kvst = sb(ph, "a_kvst", [128, 2, 256], F32)
            r_kvst = [Res(), Res()]
            pe_ = sb(ph, "a_p", [128, 2, 2, 256], BF16)
            fin = sb(ph, "a_fin", [128, 8], F32)
            tt_ = sb(ph, "a_t", [128, 128], F32)
            att = sb(ph, "a_att", [128, nseq * (QN // 128), 128], F32)
            r_att = [Res() for _ in range(nseq * (QN // 128))]
            hn = mk_hnb(ph)
            r_q, r_k, r_v, r_fin = Res(), Res(), Res(), Res()
            r_p = [Res(), Res()]
            if rope:
                cq = sb(ph, "a_cq", [128, QN], F32)
                sq_ = sb(ph, "a_sq", [128, QN], F32)
                ck = sb(ph, "a_ck", [128, 512], F32)
                sk = sb(ph, "a_sk", [128, 512], F32)
                xb = sb(ph, "a_xb", [128, 512], BF16)
                t1 = sb(ph, "a_t1", [128, 512], F32)
                t2 = sb(ph, "a_t2", [128, 512], F32)
                r_cq, r_ck, r_rp = Res(), Res(), Res()
                P.dma("sp", cq[:], c_cosq[:, :], writes=[r_cq])
                P.dma("sp", sq_[:], c_sinq[:, :], writes=[r_cq])

                def rope_block(pb, pr, n, cos_ap, sin_ap, cres, dsts):
                    CP("act", xb[:, :n], pb[:, :n], [pr], [r_rp])
                    pb2, pr2 = next_bank()
                    MM(pb2[:, :n], cRot[:], xb[:, :n], True, True, [r_rp, r_const], [pr2])
                    TT("dve", t1[:, :n], pb[:, :n], cos_ap, ALU.mult, [pr, cres], [r_rp])
                    TT("dve", t2[:, :n], pb2[:, :n], sin_ap, ALU.mult, [pr2, cres], [r_rp])
                    for dst, ps_, dres in dsts:
                        TT("pool", dst, t1[ps_, :n], t2[ps_, :n], ALU.add, [r_rp], [dres])
            for i in range(nseq * NKT):
                MSET("pool", v1h[:, i, 128:130], 1.0, [r_v])
            MSET("pool", qT[64:128, :], 0.0, [r_q])
            MSET("pool", qT2[0:64, :], 0.0, [r_q])
            stc = [0]
            def load_head_w(h):
                wt, wr = wring[wctr[0] % NWB], wres[wctr[0] % NWB]
                wctr[0] += 1
                wv = wview(wt, KC, 384)
                for j in range(3):
                    wpart(wv[:, :, j * 128:(j + 1) * 128], od_w_in[:, j * D + h * 128:j * D + (h + 1) * 128], KC, 128, wr)
                return wv, wr

            next_w = load_head_w(0)
            for h in range(8):
                wv, wr = next_w
                if h + 1 < 8:
                    next_w = load_head_w(h + 1)
                if not is_prompt:
                    wpart(kT[:, 0:PAST].rearrange("p (k n) -> p k n", k=1), ckT[h, :, :], 1, PAST, r_k)
                    for kt in range(PAST // 128):
                        wpart(v1h[:, kt:kt + 1, 0:128], cv[kt * 128:(kt + 1) * 128, h * 128:(h + 1) * 128], 1, 128, r_v)
                for seq in range(nseq):
                    T0 = seq * S
                    qoff, kbase, vt0 = seq * QN, seq * NKT * 128, seq * NKT
                    qsrc, qres_, Q0 = (hT, hres, T0) if hTq is None else (hTq, hqres, 0)
                    for tb in range(0, QN, 512):
                        n = min(512, QN - tb)
                        pb, pr = next_bank()
                        for kc in range(KC):
                            MM(pb[:, :n], wv[:, kc, 0:128], qsrc[:, kc, Q0 + tb:Q0 + tb + n],
                               kc == 0, kc == KC - 1, [wr, qres_], [pr])
                        if rope:
                            rope_block(pb, pr, n, cq[:, tb:tb + n], sq_[:, tb:tb + n], r_cq,
                                       [(qT[0:64, qoff + tb:qoff + tb + n], slice(0, 64), r_q),
                                        (qT2[64:128, qoff + tb:qoff + tb + n], slice(64, 128), r_q)])
                        else:
                            CP("act", qT[0:64, qoff + tb:qoff + tb + n], pb[0:64, :n], [pr], [r_q])
                            CP("act", qT2[64:128, qoff + tb:qoff + tb + n], pb[64:128, :n], [pr], [r_q])
                    for tb in range(0, S, 512):
                        n = min(512, S - tb)
                        pb, pr = next_bank()
                        for kc in range(KC):
                            MM(pb[:, :n], wv[:, kc, 128:256], hT[:, kc, T0 + tb:T0 + tb + n],
                               kc == 0, kc == KC - 1, [wr, hres], [pr])
                        kd = kT[:, kbase + koff * 128 + tb:kbase + koff * 128 + tb + n]
                        if rope:
                            P.dma("sp", ck[:, :n], c_cosk[:, tb:tb + n], writes=[r_ck])
                            P.dma("sp", sk[:, :n], c_sink[:, tb:tb + n], writes=[r_ck])
                            rope_block(pb, pr, n, ck[:, :n], sk[:, :n], r_ck, [(kd, slice(0, 128), r_k)])
                        else:
                            CP("act", kd, pb[:, :n], [pr], [r_k])
                    for i in range(NT):
                        tk = slice(T0 + i * 128, T0 + (i + 1) * 128)
                        pb, pr = next_bank()
                        for kc in range(KC):
                            MM(pb[:, 0:256], hT[:, kc, tk], wv[:, kc, 128:384], kc == 0, kc == KC - 1, [wr, hres], [pr])
                        si = stc[0] % 2
                        stc[0] += 1
                        CP("act", kvst[:, si, :], pb[:, 0:256], [pr], [r_kvst[si]])
                        CP("dve", v1h[:, vt0 + koff + i, 0:128], kvst[:, si, 128:256], [r_kvst[si]], [r_v])
                        if is_prompt:
                            P.dma("sp", o_k[T0 + i * 128:T0 + (i + 1) * 128, h * 128:(h + 1) * 128], kvst[:, si, 0:128],
                                  reads=[r_kvst[si]], is_output=True)
                            P.dma("sp", o_v[T0 + i * 128:T0 + (i + 1) * 128, h * 128:(h + 1) * 128], kvst[:, si, 128:256],
                                  reads=[r_kvst[si]], is_output=True)
                hitems = []
                for seq in range(nseq):
                    T0 = seq * S
                    qoff, kbase, vt0 = seq * QN, seq * NKT * 128, seq * NKT
                    Q0 = T0 if hTq is None else 0
                    for qb in range(0, QN, 256):
                        acc = [(pbank[0], pres[0]), (pbank[1], pres[1])]
                        def emit_scores(kt, qb=qb, qoff=qoff, kbase=kbase):
                            ks = slice(kbase + kt * 128, kbase + (kt + 1) * 128)
                            b0 = 2 if kt % 2 == 0 else 4
                            MM(pbank[b0][:, 0:256], kT[:, ks], qT[:, qoff + qb:qoff + qb + 256], True, True, [r_k, r_q],
                               [pres[b0]])
                            MM(pbank[b0 + 1][:, 0:256], kT[:, ks], qT2[:, qoff + qb:qoff + qb + 256], True, True, [r_k, r_q],
                               [pres[b0 + 1]])
                        emit_scores(0)
                        for kt in range(NKT):
                            b0 = 2 if kt % 2 == 0 else 4
                            sa, sar = pbank[b0], pres[b0]
                            sb2, sbr = pbank[b0 + 1], pres[b0 + 1]
                            pi = kt % 2
                            if kt + 1 < NKT:
                                emit_scores(kt + 1)
                            ACT(pe_[:, pi, 0, :], sa[:, 0:256], AF.Exp, [sar], [r_p[pi]], scale=0.125)
                            ACT(pe_[:, pi, 1, :], sb2[:, 0:256], AF.Exp, [sbr], [r_p[pi]], scale=0.125)
                            for qs in range(2):
                                ab, ar = acc[qs]
                                MM(ab[:, 0:129], pe_[:, pi, 0, qs * 128:(qs + 1) * 128], v1h[:, vt0 + kt, 0:129],
                                   kt == 0, False, [r_p[pi], r_v], [ar])
                                MM(ab[:, 129:258], pe_[:, pi, 1, qs * 128:(qs + 1) * 128], v1h[:, vt0 + kt, 0:129],
                                   False, kt == NKT - 1, [r_p[pi], r_v], [ar])
                        for qs in range(2):
                            ab, ar = acc[qs]
                            RECIP(fin[:, 0:1], ab[:, 128:129], [ar], [r_fin])
                            RECIP(fin[:, 1:2], ab[:, 257:258], [ar], [r_fin])
                            TT("dve", fin[:, 2:3], fin[:, 1:2], lamv[:, 1:2], ALU.mult, [r_fin, r_par], [r_fin])
                            TS("dve", tt_[:], ab[:, 129:257], fin[:, 2:3], ALU.mult, [ar, r_fin], [r_fin])
                            ai = seq * (QN // 128) + qb // 128 + qs
                            STT(att[:, ai, :], ab[:, 0:128], fin[:, 0:1], tt_[:], ALU.mult, ALU.add, [ar, r_fin, r_att[ai]],
                                [r_att[ai]])
                            tk = slice(Q0 + qb + qs * 128, Q0 + qb + (qs + 1) * 128)
                            hitems.append((att[:, ai, :], dagS[:], None, moT[:, h, tk], [r_att[ai]], mores))
                head_norm_batch(hn, hitems)

    def odd_mixer_prompt(seg, hT, hres, moT, mores):
        odd_mixer(seg, hT, hres, moT, mores, PSEQ, PS, True)

    def segment_sample():
        w = 1
        T = SS
        with phase() as seg:
            buf2 = sb(seg, "buf2", [128, KC, T], BF16)
            selT = sb(seg, "selT", [128, 4], F32)
            r_own, r_hq, r_moq, r_b2, r_sel = Res(), Res(), Res(), Res(), Res()
            P.dma("sp", selT[:], sel[:, :], writes=[r_sel])
            with phase() as H:
                hT = sb(H, "s_hT", [128, KC, T], BF16)
                hres = Res()
                with phase() as Ln:
                    xt = sb(Ln, "s_xt", [128, KC, 512], F32)
                    r_xt = Res()
                    scr = mk_norm_scr(Ln)
                    for t0 in range(0, T, 512):
                        for kc in range(KC):
                            P.dma("sp", xt[:, kc, :], xsT[kc * 128:(kc + 1) * 128, t0:t0 + 512], writes=[r_xt])
                        norm_mod(scr, xt, r_xt, 0, 512, gm[:, 0, 0, w, :], mvec(0, 0, w), hT, t0, hres)
                even_mixer(H, hT, hres, buf2, r_b2, 1, SS, False, SP=512)
            x1own = sb(seg, "x1own", [128, KC, NQ], F32)
            hTq = sb(seg, "hTq", [128, KC, NQ], BF16)
            with phase() as X:
                x1T = sb(X, "x1T", [128, KC, T], F32)
                r_x1 = Res()
                load_xT(x1T, r_x1, xsT, T)
                linear_fm(ev_w_out, KC, D, buf2, r_b2, T, residual_epilogue(x1T, r_x1, mvec(0, 2, w)))
                with phase() as Ln:
                    scr = mk_norm_scr(Ln)
                    for t0 in range(0, T, 512):
                        norm_mod(scr, x1T, r_x1, t0, 512, gm[:, 0, 1, w, :], mvec(0, 3, w), buf2, t0, r_b2)
                with phase() as F_:
                    ffn(F_, 0, buf2, r_b2, T, x1T, r_x1, mvec(0, 5, w))
                with phase() as Ln:
                    scr = mk_norm_scr(Ln)
                    for t0 in range(0, T, 512):
                        norm_mod(scr, x1T, r_x1, t0, 512, gm[:, 1, 0, w, :], mvec(1, 0, w), buf2, t0, r_b2)
                for kc in range(KC):
                    for j in range(4):
                        js = slice(j * NQ, (j + 1) * NQ)
                        if j == 0:
                            TS("dve", x1own[:, kc, :], x1T[:, kc, js], selT[:, 0:1], ALU.mult, [r_x1, r_sel], [r_own])
                            TS("pool", hTq[:, kc, :], buf2[:, kc, js], selT[:, 0:1], ALU.mult, [r_b2, r_sel], [r_hq])
                        else:
                            STT(x1own[:, kc, :], x1T[:, kc, js], selT[:, j:j + 1], x1own[:, kc, :], ALU.mult, ALU.add,
                                [r_x1, r_sel, r_own], [r_own])
                            STT(hTq[:, kc, :], buf2[:, kc, js], selT[:, j:j + 1], hTq[:, kc, :], ALU.mult, ALU.add,
                                [r_b2, r_sel, r_hq], [r_hq])
            moTq = sb(seg, "moTq", [128, KC, NQ], BF16)
            odd_mixer(seg, buf2, r_b2, moTq, r_moq, 1, SS, False, hTq=hTq, hqres=r_hq, SQ=NQ)
            linear_fm(od_w_out, KC, D, moTq, r_moq, NQ, residual_epilogue(x1own, r_own, mvec(1, 2, w)))
            with phase() as Ln:
                scr = mk_norm_scr(Ln)
                norm_mod(scr, x1own, r_own, 0, NQ, gm[:, 1, 1, w, :], mvec(1, 3, w), hTq, 0, r_hq)
            with phase() as F_:
                ffn(F_, 1, hTq, r_hq, NQ, x1own, r_own, mvec(1, 5, w))
            with phase() as Ln:
                scr = mk_norm_scr(Ln)
                final_norm_out(Ln, scr, x1own, r_own, NQ, ysT)


    class phase:
        def __enter__(self):
            self.st = ExitStack()
            return self.st
        def __exit__(self, *a):
            P.barrier()
            self.st.close()
            return False

    def load_xT(dst, dres, src, T):
        for kc in range(KC):
            P.dma("sp", dst[:, kc, :T], src[kc * 128:(kc + 1) * 128, 0:T], writes=[dres])

    def stub_mixer(hT, hres, moT, mores, T):
        for kc in range(KC):
            P.op("pool", lambda e, kc=kc: e.tensor_copy(out=moT[:, kc, :T], in_=hT[:, kc, :T]), [hres], [mores])

    def final_norm_out(seg, scr, xT, xres, T, out_ap):
        fin = sb(seg, "fin", [128, KC], F32)
        r_fin = Res()
        P.dma("sp", fin[:], finT[:, :], writes=[r_fin])
        yst = sb(seg, "yst", [128, KC, 512], F32)
        r_y = Res()
        for t0 in range(0, T, 512):
            norm_mod(scr, xT, xres, t0, 512, fin, None, yst, 0, r_y, extra_reads=[r_fin])
            for kc in range(KC):
                P.dma("sp", out_ap[kc * 128:(kc + 1) * 128, t0:t0 + 512], yst[:, kc, :], reads=[r_y], is_output=True)

    def segment_prompt():
        T = PSEQ * PS
        w = 0
        with phase() as seg:
            xT = sb(seg, "xT", [128, KC, T], F32)
            hT = sb(seg, "hT", [128, KC, T], BF16)
            moT = sb(seg, "moT", [128, KC, T], BF16)
            xres, hres, mores = Res(), Res(), Res()
            load_xT(xT, xres, xpT, T)
            scr = mk_norm_scr(seg)
            for l in range(2):
                for t0 in range(0, T, 512):
                    norm_mod(scr, xT, xres, t0, 512, gm[:, l, 0, w, :], mvec(l, 0, w), hT, t0, hres)
                if l == 0:
                    if STAGE >= 2:
                        even_mixer(seg, hT, hres, moT, mores, PSEQ, PS, True)
                    else:
                        stub_mixer(hT, hres, moT, mores, T)
                    if DEBUG:
                        for kc in range(KC):
                            P.dma("sp", dbgT[kc * 128:(kc + 1) * 128, :], moT[:, kc, :], reads=[mores], is_output=True)
                    linear_fm(ev_w_out, KC, D, moT, mores, T, residual_epilogue(xT, xres, mvec(l, 2, w)))
                else:
                    if STAGE >= 3:
                        odd_mixer_prompt(seg, hT, hres, moT, mores)
                    else:
                        stub_mixer(hT, hres, moT, mores, T)
                    linear_fm(od_w_out, KC, D, moT, mores, T, residual_epilogue(xT, xres, mvec(l, 2, w)))
                for t0 in range(0, T, 512):
                    norm_mod(scr, xT, xres, t0, 512, gm[:, l, 1, w, :], mvec(l, 3, w), hT, t0, hres)
                with phase() as ph:
                    ffn(ph, l, hT, hres, T, xT, xres, mvec(l, 5, w), TILE=1024)
            final_norm_out(seg, scr, xT, xres, T, ypT)

    phase_adaln()
    phase_params()
    segment_prompt()
    if STAGE >= 4:
        segment_sample()
    _LAST['P'] = P
    _LAST['nwc'] = len(wcache)
    P.finish()
    return nc


def _consts():
    i = np.arange(128)
    s, t = i[:, None], i[None, :]
    same32 = (s // 32) == (t // 32)
    cf = np.zeros((128, 7, 128), np.float32)
    cf[:, 0] = np.eye(128)
    cf[:, 1] = (s <= t)
    cf[:, 2] = (s >= t)
    cf[:, 3] = 1.0
    cf[:, 4] = same32 & (s > t)
    cf[:, 5] = same32 & (s < t)
    cf[:, 6, 0:4] = (s // 32) == np.arange(4)[None, :]
    cb = np.zeros((128, 6, 128), np.float32)
    cb[:, 0] = np.eye(128)
    cb[:, 1] = 1.0
    cb[:, 2] = (s <= t)
    cb[:, 3] = (s >= t)
    cb[:, 4] = same32 & (s <= t)
    cb[:, 5] = same32 & (s >= t)
    R = np.zeros((128, 128), np.float32)
    for d in range(128):
        if d % 32 < 16:
            R[d, d + 16] = -1.0
        else:
            R[d, d - 16] = 1.0
    rot = np.ascontiguousarray(R.T)
    return cf, cb, rot


def _rope_tables(positions):
    d = np.arange(128)
    inv = (10000.0 ** (-(np.arange(16, dtype=np.float32)) / 16.0)).astype(np.float32)
    row = (positions // 64).astype(np.float32)
    col = (positions % 64).astype(np.float32)
    pos = np.where(((d % 64) < 32)[:, None], row[None, :], col[None, :]).astype(np.float32)
    ang = (pos * inv[d % 16][:, None]).astype(np.float32)
    return np.cos(ang).astype(np.float32), np.sin(ang).astype(np.float32)


_NC_CACHE = {}


def _get_nc():
    if "nc" not in _NC_CACHE:
        _NC_CACHE["nc"] = build_program()
    return _NC_CACHE["nc"]


def make_in_maps(x_prompt, x_sample, c, c_ctx, cache_attn_k, cache_attn_v, state_mlstm_C, state_mlstm_n,
                 state_mlstm_m, state_hgrn_S, ada_w, ada_b, norm_mix_g, norm_ffn_g, ev_w_in, ev_gate_b,
                 ev_lb_logits, ml_norm_g, hg_norm_g, ev_w_out, od_w_in, od_lambda, da_norm_g, od_w_out,
                 ffn_w1, ffn_w3, ffn_w2, final_norm_g):
    f = lambda a: np.ascontiguousarray(np.asarray(a, dtype=np.float32))
    cf, cb, rot = _consts()
    cosk, sink = _rope_tables(np.arange(SS))
    shared = {
        "ada_w": f(ada_w),
        "ada_bT": f(np.asarray(ada_b).reshape(2, 48, 128).transpose(2, 0, 1)),
        "nmixT": f(np.asarray(norm_mix_g).reshape(2, KC, 128).transpose(2, 0, 1)),
        "nffnT": f(np.asarray(norm_ffn_g).reshape(2, KC, 128).transpose(2, 0, 1)),
        "finT": f(np.asarray(final_norm_g).reshape(KC, 128).T),
        "ev_w_in": f(np.asarray(ev_w_in)[0]),
        "gate_b": f(np.broadcast_to(np.asarray(ev_gate_b)[0][None, :], (128, 16))),
        "lb_log": f(np.broadcast_to(np.asarray(ev_lb_logits)[None, :, :], (128, 2, 512))),
        "mlg_g": f(np.broadcast_to(np.asarray(ml_norm_g)[0][None, :], (128, 512))),
        "hgg_g": f(np.broadcast_to(np.asarray(hg_norm_g)[0][None, :], (128, 512))),
        "ev_w_out": f(np.asarray(ev_w_out)[0]),
        "od_w_in": f(np.asarray(od_w_in)[0]),
        "od_lam": f(np.broadcast_to(np.asarray(od_lambda)[0][None, :, :], (128, 4, 64))),
        "da_g": f(np.broadcast_to(np.asarray(da_norm_g)[0][None, :], (128, 128))),
        "od_w_out": f(np.asarray(od_w_out)[0]),
        "ffn_w1": f(ffn_w1), "ffn_w3": f(ffn_w3), "ffn_w2": f(ffn_w2),
        "c_cosk": cosk, "c_sink": sink, "c_f32": cf, "c_bf": cb, "c_rot": rot,
    }
    xp = np.asarray(x_prompt, dtype=np.float32)
    xs = np.asarray(x_sample, dtype=np.float32)
    maps = []
    for core in range(NCORE):
        b, j = core // 4, core % 4
        m = dict(shared)
        m["xpT"] = f(xp[PSEQ * core:PSEQ * (core + 1)].reshape(PSEQ * PS, D).T)
        m["xsT"] = f(xs[b].T)
        cond = np.stack([np.asarray(c_ctx, np.float32), np.asarray(c, np.float32)[b]], axis=-1)
        m["condT"] = f(cond.reshape(KC, 128, 2).transpose(1, 0, 2))
        m["ckT"] = f(np.asarray(cache_attn_k)[b, 0].transpose(1, 2, 0))
        m["cv"] = f(np.asarray(cache_attn_v)[b, 0].reshape(PAST, D))
        m["st_C"] = f(np.asarray(state_mlstm_C)[b, 0])
        m["st_n"] = f(np.asarray(state_mlstm_n)[b, 0].reshape(8, 128))
        m["st_m"] = f(np.broadcast_to(np.asarray(state_mlstm_m)[b, 0].reshape(8, 1), (8, 128)))
        m["st_S"] = f(np.asarray(state_hgrn_S)[b, 0])
        selv = np.zeros((128, 4), np.float32)
        selv[:, j] = 1.0
        m["sel"] = selv
        cq, sq = _rope_tables(np.arange(j * NQ, (j + 1) * NQ))
        m["c_cosq"], m["c_sinq"] = cq, sq
        maps.append(m)
    return maps


def assemble(results):
    B = NCORE * PSEQ
    y_prompt = np.zeros((B, PS, D), np.float32)
    y_sample = np.zeros((2, SS, D), np.float32)
    nk = np.zeros((B, 1, PS, 8, 128), np.float32)
    nv = np.zeros((B, 1, PS, 8, 128), np.float32)
    nC = np.zeros((B, 1, 2, 4, 128, 128), np.float32)
    nn = np.zeros((B, 1, 2, 4, 128), np.float32)
    nm = np.zeros((B, 1, 2, 4), np.float32)
    nS = np.zeros((B, 1, 2, 4, 128, 128), np.float32)
    for core in range(NCORE):
        r = results[core]
        b, j = core // 4, core % 4
        sl = slice(PSEQ * core, PSEQ * (core + 1))
        y_prompt[sl] = np.asarray(r["ypT"]).T.reshape(PSEQ, PS, D)
        y_sample[b, j * NQ:(j + 1) * NQ] = np.asarray(r["ysT"]).T
        nk[sl, 0] = np.asarray(r["o_k"]).reshape(PSEQ, PS, 8, 128)
        nv[sl, 0] = np.asarray(r["o_v"]).reshape(PSEQ, PS, 8, 128)
        nC[sl, 0] = np.asarray(r["o_C"])
        nn[sl, 0] = np.asarray(r["o_n"])
        nm[sl, 0] = np.asarray(r["o_m"])
        nS[sl, 0] = np.asarray(r["o_S"])
    return (y_prompt, y_sample, nk, nv, nC, nn, nm, nS)


def kernel(**inputs):
    nc = _get_nc()
    maps = make_in_maps(**inputs)
    res = run_bass_kernel_spmd(nc, maps, core_ids=list(range(NCORE)))
    _LAST["res"] = res.results
    return assemble(res.results)
```
